# Optimizing a Trainium2 kernel written in Bass

```python
import math
import jax, jax.numpy as jnp
from jax import lax
import numpy as np

D_MODEL = 1024
BATCH = 8
SEQ = 2048
DEPTH = 2
DEC_BATCH = 32
DEC_SEQ = 1
PAST_LEN = 16384
PAGE_SIZE = 128

GROUP_W = D_MODEL // 4
N_PROJ = 9 * GROUP_W
CONV_A_WIDTH = 31
HEAD_DIM = 64
N_HEADS_B = GROUP_W // HEAD_DIM
DIL_CONFIGS = ((128, 1), (512, 4), (2048, 16))
MAX_WINDOW = max(w for w, _ in DIL_CONFIGS)
Q_BLOCK = 128
ATTN_SCALE = 1.0 / math.sqrt(HEAD_DIM)
CONV_C_WIDTH = 3
POOL_WINDOWS = (2, 4, 8, 16)
POOL_GROUP = GROUP_W // len(POOL_WINDOWS)
POOL_STATE = max(POOL_WINDOWS) - 1
FF_DIM = -(-8 * D_MODEL // (3 * 256)) * 256
EPS = 1e-6
NEG = -1e30

kernel_name = "hybrid_conformer_dilattn_shortconv_pool_step"


def rmsnorm(x, g):
    xf = x.astype(jnp.float32)
    y = xf * lax.rsqrt(jnp.mean(xf * xf, axis=-1, keepdims=True) + EPS)
    return (y * g.astype(jnp.float32)).astype(x.dtype)


def layernorm(x, g, b):
    xf = x.astype(jnp.float32)
    mu = jnp.mean(xf, axis=-1, keepdims=True)
    xc = xf - mu
    y = xc * lax.rsqrt(jnp.mean(xc * xc, axis=-1, keepdims=True) + EPS)
    return (y * g.astype(jnp.float32) + b.astype(jnp.float32)).astype(x.dtype)


def depthwise_causal_conv(xp, w):
    c = xp.shape[-1]
    return lax.conv_general_dilated(xp, w[:, None, :].astype(xp.dtype), window_strides=(1,), padding="VALID",
                                    dimension_numbers=("NWC", "WIO", "NWC"), feature_group_count=c)


def multi_pool(up, pos0):
    n, tot, c = up.shape
    t = tot - POOL_STATE
    uf = up.astype(jnp.float32)
    cs = jnp.concatenate([jnp.zeros((n, 1, c), jnp.float32), jnp.cumsum(uf, axis=1)], axis=1)
    pos = pos0 + jnp.arange(t)
    p1 = POOL_STATE + 1
    outs = []
    for g, w in enumerate(POOL_WINDOWS):
        sl = slice(g * POOL_GROUP, (g + 1) * POOL_GROUP)
        s = cs[:, p1:p1 + t, sl] - cs[:, p1 - w:p1 - w + t, sl]
        cnt = jnp.minimum(w, pos + 1).astype(jnp.float32)[None, :, None]
        outs.append(s / cnt)
    return (jnp.concatenate(outs, axis=-1) - uf[:, POOL_STATE:]).astype(up.dtype)


def band_attn(q, k, v, n_back):
    n, l, h, hd = q.shape
    nb = -(-l // Q_BLOCK)
    lp = nb * Q_BLOCK
    qb = jnp.pad(q, [(0, 0), (0, lp - l), (0, 0), (0, 0)]).reshape(n, nb, Q_BLOCK, h, hd)

    def windows(a):
        ab = jnp.pad(a, [(0, 0), (Q_BLOCK, lp - l), (0, 0), (0, 0)]).reshape(n, nb + 1, Q_BLOCK, h, hd)
        return jnp.concatenate([ab[:, :-1], ab[:, 1:]], axis=2)

    kw, vw = windows(k), windows(v)
    s = jnp.einsum("nbqhd,nbkhd->nbhqk", qb.astype(jnp.float32), kw.astype(jnp.float32)) * ATTN_SCALE
    qi = jnp.arange(Q_BLOCK)[:, None]
    ki = jnp.arange(2 * Q_BLOCK)[None, :] - Q_BLOCK
    rel = qi - ki
    kpos = jnp.arange(nb)[:, None, None] * Q_BLOCK + ki[None]
    mask = (rel >= 0)[None] & (rel <= n_back)[None] & (kpos >= 0)
    s = jnp.where(mask[None, :, None], s, NEG)
    m = jnp.max(s, axis=-1, keepdims=True)
    p = jnp.exp(s - m)
    den = jnp.sum(p, axis=-1)
    o = jnp.einsum("nbhqk,nbkhd->nbqhd", p, vw.astype(jnp.float32))
    den_t = jnp.moveaxis(den, 2, 3)
    lse_t = jnp.moveaxis(m[..., 0] + jnp.log(den), 2, 3)
    o = o / den_t[..., None]
    return o.reshape(n, lp, h, hd)[:, :l], lse_t.reshape(n, lp, h)[:, :l]


def combine_by_denominator(outs, lses):
    w = jax.nn.softmax(jnp.stack(lses, axis=0), axis=0)
    return jnp.einsum("cnth,cnthd->nthd", w, jnp.stack(outs, axis=0))


def dilated_attn_prompt(q, k, v):
    n, s, h, hd = q.shape
    outs, lses = [], []
    for window, dil in DIL_CONFIGS:
        l = s // dil

        def to_sub(a):
            return a.reshape(n, l, dil, h, hd).transpose(0, 2, 1, 3, 4).reshape(n * dil, l, h, hd)

        o, lse = band_attn(to_sub(q), to_sub(k), to_sub(v), window // dil)
        outs.append(o.reshape(n, dil, l, h, hd).transpose(0, 2, 1, 3, 4).reshape(n, s, h, hd))
        lses.append(lse.reshape(n, dil, l, h).transpose(0, 2, 1, 3).reshape(n, s, h))
    return combine_by_denominator(outs, lses)


def dilated_attn_sample(q, kc, vc, buf_len):
    t = q.shape[1]
    qf = q.astype(jnp.float32)
    outs, lses = [], []
    for window, dil in DIL_CONFIGS:
        nk = window // dil + 1
        idx = buf_len + jnp.arange(t)[:, None] - jnp.arange(nk)[None, :] * dil
        valid = idx >= 0
        idc = jnp.maximum(idx, 0)
        kg = jnp.take(kc, idc, axis=1).astype(jnp.float32)
        vg = jnp.take(vc, idc, axis=1).astype(jnp.float32)
        s = jnp.einsum("nthd,ntkhd->nthk", qf, kg) * ATTN_SCALE
        s = jnp.where(valid[None, :, None, :], s, NEG)
        m = jnp.max(s, axis=-1, keepdims=True)
        p = jnp.exp(s - m)
        den = jnp.sum(p, axis=-1)
        o = jnp.einsum("nthk,ntkhd->nthd", p, vg) / den[..., None]
        outs.append(o)
        lses.append(m[..., 0] + jnp.log(den))
    return combine_by_denominator(outs, lses)


def hybrid_mixer(h, pos0, kbuf, vbuf, abuf, cbuf, pbuf, w_in, conv_a_w, conv_a_b, ln_a_g, ln_a_b,
                 conv_c_w, pool_w, pool_scale, w_out):
    n, t, _ = h.shape
    proj = h @ w_in
    a_val, a_gate, q, k, v, c_x, c_b, c_c, d_u = jnp.split(proj, 9, axis=-1)
    ga = a_val * jax.nn.sigmoid(a_gate)
    ap = jnp.concatenate([abuf, ga], axis=1)
    ya = depthwise_causal_conv(ap, conv_a_w) + conv_a_b
    ya = jax.nn.silu(layernorm(ya, ln_a_g, ln_a_b))
    new_a = ap[:, -(CONV_A_WIDTH - 1):]
    q = q.reshape(n, t, N_HEADS_B, HEAD_DIM)
    k = k.reshape(n, t, N_HEADS_B, HEAD_DIM)
    v = v.reshape(n, t, N_HEADS_B, HEAD_DIM)
    if kbuf is None:
        ob = dilated_attn_prompt(q, k, v)
        keep = min(MAX_WINDOW, t)
        new_k, new_v = k[:, t - keep:], v[:, t - keep:]
    else:
        buf_len = kbuf.shape[1]
        kc = jnp.concatenate([kbuf, k], axis=1)
        vc = jnp.concatenate([vbuf, v], axis=1)
        ob = dilated_attn_sample(q, kc, vc, buf_len)
        new_k, new_v = kc[:, -buf_len:], vc[:, -buf_len:]
    ob = ob.reshape(n, t, GROUP_W).astype(h.dtype)
    cp = jnp.concatenate([cbuf, c_c * c_x], axis=1)
    yc = c_b * depthwise_causal_conv(cp, conv_c_w)
    new_c = cp[:, -(CONV_C_WIDTH - 1):]
    pp = jnp.concatenate([pbuf, d_u], axis=1)
    yd = multi_pool(pp, pos0).reshape(n, t, len(POOL_WINDOWS), POOL_GROUP)
    yd = jnp.einsum("ntgc,gce->ntge", yd, pool_w).reshape(n, t, GROUP_W) * pool_scale
    new_p = pp[:, -POOL_STATE:]
    mix = jnp.concatenate([ya, ob, yc, yd], axis=-1) @ w_out
    return mix, (new_k, new_v, new_a, new_c, new_p)


def swiglu(h, w_gu, w_down):
    g, u = jnp.split(h @ w_gu, 2, axis=-1)
    return (jax.nn.silu(g) * u) @ w_down


def trunk(x, pos0, caches, w_in, conv_a_w, conv_a_b, ln_a_g, ln_a_b, conv_c_w, pool_w, pool_scale,
          w_out, norm1_g, norm2_g, w_gu, w_down, final_g):
    n = x.shape[0]
    new = ([], [], [], [], [])
    for l in range(DEPTH):
        if caches is None:
            kb = vb = None
            ab = jnp.zeros((n, CONV_A_WIDTH - 1, GROUP_W), x.dtype)
            cb = jnp.zeros((n, CONV_C_WIDTH - 1, GROUP_W), x.dtype)
            pb = jnp.zeros((n, POOL_STATE, GROUP_W), x.dtype)
        else:
            kb, vb, ab, cb, pb = caches[0][l], caches[1][l], caches[2][l], caches[3][l], caches[4][l]
        h = rmsnorm(x, norm1_g[l])
        mix, st = hybrid_mixer(h, pos0, kb, vb, ab, cb, pb, w_in[l], conv_a_w[l], conv_a_b[l], ln_a_g[l],
                               ln_a_b[l], conv_c_w[l], pool_w[l], pool_scale[l], w_out[l])
        x = x + mix
        x = x + swiglu(rmsnorm(x, norm2_g[l]), w_gu[l], w_down[l])
        for lst, s in zip(new, st):
            lst.append(s)
    y = rmsnorm(x, final_g)
    return y, [jnp.stack(s, axis=0) for s in new]


def setup_inputs(seed: int = 0) -> dict:
    key = jax.random.key(seed)
    ks = jax.random.split(key, 24)
    f32 = jnp.float32
    buf_s = min(MAX_WINDOW, PAST_LEN)
    nrm = lambda k, shape, sc: jax.random.normal(k, shape, f32) * sc
    return {
        "x_prompt": nrm(ks[0], (BATCH, SEQ, D_MODEL), 1.0),
        "x_sample": nrm(ks[1], (DEC_BATCH, DEC_SEQ, D_MODEL), 1.0),
        "cache_win_k": nrm(ks[2], (DEPTH, DEC_BATCH, buf_s, N_HEADS_B, HEAD_DIM), 1.0),
        "cache_win_v": nrm(ks[3], (DEPTH, DEC_BATCH, buf_s, N_HEADS_B, HEAD_DIM), 1.0),
        "state_conv_a": nrm(ks[4], (DEPTH, DEC_BATCH, CONV_A_WIDTH - 1, GROUP_W), 0.5),
        "state_conv_c": nrm(ks[5], (DEPTH, DEC_BATCH, CONV_C_WIDTH - 1, GROUP_W), 1.0),
        "state_pool": nrm(ks[6], (DEPTH, DEC_BATCH, POOL_STATE, GROUP_W), 1.0),
        "w_in": nrm(ks[7], (DEPTH, D_MODEL, N_PROJ), D_MODEL ** -0.5),
        "conv_a_w": nrm(ks[8], (DEPTH, CONV_A_WIDTH, GROUP_W), CONV_A_WIDTH ** -0.5),
        "conv_a_b": nrm(ks[9], (DEPTH, GROUP_W), 0.02),
        "ln_a_g": 1.0 + nrm(ks[10], (DEPTH, GROUP_W), 0.02),
        "ln_a_b": nrm(ks[11], (DEPTH, GROUP_W), 0.02),
        "conv_c_w": nrm(ks[12], (DEPTH, CONV_C_WIDTH, GROUP_W), CONV_C_WIDTH ** -0.5),
        "pool_w": nrm(ks[13], (DEPTH, len(POOL_WINDOWS), POOL_GROUP, POOL_GROUP), POOL_GROUP ** -0.5),
        "pool_scale": 1.0 + nrm(ks[14], (DEPTH, GROUP_W), 0.02),
        "w_out": nrm(ks[15], (DEPTH, D_MODEL, D_MODEL), D_MODEL ** -0.5),
        "norm1_g": 1.0 + nrm(ks[16], (DEPTH, D_MODEL), 0.02),
        "norm2_g": 1.0 + nrm(ks[17], (DEPTH, D_MODEL), 0.02),
        "w_gu": nrm(ks[18], (DEPTH, D_MODEL, 2 * FF_DIM), D_MODEL ** -0.5),
        "w_down": nrm(ks[19], (DEPTH, FF_DIM, D_MODEL), FF_DIM ** -0.5),
        "final_g": 1.0 + nrm(ks[20], (D_MODEL,), 0.02),
    }


def reference(x_prompt, x_sample, cache_win_k, cache_win_v, state_conv_a, state_conv_c, state_pool,
              w_in, conv_a_w, conv_a_b, ln_a_g, ln_a_b, conv_c_w, pool_w, pool_scale, w_out,
              norm1_g, norm2_g, w_gu, w_down, final_g):
    y_prompt, st_p = trunk(x_prompt, 0, None, w_in, conv_a_w, conv_a_b, ln_a_g, ln_a_b, conv_c_w, pool_w,
                           pool_scale, w_out, norm1_g, norm2_g, w_gu, w_down, final_g)
    caches = (cache_win_k, cache_win_v, state_conv_a, state_conv_c, state_pool)
    y_sample, st_s = trunk(x_sample, PAST_LEN, caches, w_in, conv_a_w, conv_a_b, ln_a_g, ln_a_b, conv_c_w,
                           pool_w, pool_scale, w_out, norm1_g, norm2_g, w_gu, w_down, final_g)
    k_p, v_p, a_p, c_p, p_p = st_p
    k_s, v_s, a_s, c_s, p_s = st_s
    return (y_prompt, y_sample, k_p, v_p, a_p, c_p, p_p, k_s, v_s, a_s, c_s, p_s)
```

```python
from contextlib import ExitStack
import numpy as np
import concourse.bass as bass
import concourse.mybir as mybir
from concourse.bass_utils import run_bass_kernel_spmd

F32 = mybir.dt.float32
BF16 = mybir.dt.bfloat16
ALU = mybir.AluOpType
AF = mybir.ActivationFunctionType
AX = mybir.AxisListType

T = 2048
NS = 4
TT = T + NS
D = 1024
NPJ = 2304
FF = 2816
EPS = 1e-6
NEGM = -30000.0
NPRM = 92
NCST = 800
W_IN_ORDER = (4, 3, 2, 0, 1, 5, 7, 8, 6)
FF_GROUPS = ((0, 1, 2), (3, 4, 5), (6, 7, 8), (9, 10))
NSTG = 3
NRING = 3
CAST_ENG = "act"
AUXQ = "pool"


class Prog:
    ENGS = ("pe", "act", "dve", "pool", "sp")

    def __init__(self, nc, es):
        self.nc = nc
        self.es = es
        self.lists = {e: [] for e in self.ENGS}
        self.sem = {}
        self.cnt = {}
        self.waited = {e: {} for e in self.ENGS}
        self.lastw = {}
        self.reads = {}
        self.n_instr = 0
        for e in self.ENGS:
            self._mksem("c_" + e)

    def _mksem(self, name):
        if name not in self.sem:
            self.sem[name] = self.es.enter_context(self.nc.semaphore(name))
            self.cnt[name] = 0
        return self.sem[name]

    def sb(self, name, shape, dt, es=None):
        return (es or self.es).enter_context(self.nc.sbuf_tensor(name, list(shape), dt))

    def ps(self, name, shape, dt=F32):
        return self.es.enter_context(self.nc.psum_tensor(name, list(shape), dt))

    def _deps(self, eng, reads, writes, extra):
        need = {}

        def add(tok):
            if tok is None:
                return
            s, v = tok
            if eng == "pe" and s == "c_pe":
                return
            if need.get(s, 0) < v:
                need[s] = v

        for k in reads:
            add(self.lastw.get(k))
        for k in writes:
            add(self.lastw.get(k))
            for t in self.reads.get(k, ()):
                add(t)
        for t in extra:
            add(t)
        out = []
        w = self.waited[eng]
        for s, v in need.items():
            if w.get(s, 0) < v:
                w[s] = v
                out.append((s, v))
        return out

    def _commit(self, tok, reads, writes):
        for k in writes:
            self.lastw[k] = tok
            self.reads[k] = []
        for k in reads:
            lst = self.reads.setdefault(k, [])
            lst.append(tok)
            if len(lst) > 10:
                m = {}
                for s, v in lst:
                    if m.get(s, 0) < v:
                        m[s] = v
                self.reads[k] = list(m.items())

    def op(self, eng, fn, reads=(), writes=(), sig=True, extra=()):
        waits = self._deps(eng, reads, writes, extra)
        sname = "c_" + eng
        if sig:
            self.cnt[sname] += 1
            tok = (sname, self.cnt[sname])
        else:
            tok = (sname, self.cnt[sname] + 1)
        self.lists[eng].append((waits, fn, (sname, 1) if sig else None))
        self._commit(tok, reads, writes)
        self.n_instr += 1
        return tok

    def dma(self, semname, fn, reads=(), writes=(), eng="sp", extra=()):
        self._mksem(semname)
        waits = self._deps(eng, reads, writes, extra)
        self.cnt[semname] += 16
        tok = (semname, self.cnt[semname])
        self.lists[eng].append((waits, fn, (semname, 16)))
        self._commit(tok, reads, writes)
        self.n_instr += 1
        return tok

    def wait_all(self, eng, toks):
        need = {}
        w = self.waited[eng]
        for s, v in toks:
            if need.get(s, 0) < v and w.get(s, 0) < v:
                need[s] = v
        for s, v in need.items():
            w[s] = v
        if need:
            self.lists[eng].append((list(need.items()), None, None))

    def drain_dmas(self, exclude=("cpy", "wst")):
        toks = [(s, v) for s, v in self.cnt.items() if not s.startswith("c_") and v > 0
                and not any(s.startswith(x) for x in exclude)]
        for e in ("pe", "act", "dve", "pool"):
            self.wait_all(e, toks)

    def emit(self):
        nc = self.nc
        lists = self.lists
        self.lists = {e: [] for e in self.ENGS}
        sem = self.sem
        with nc.Block() as block:
            def mk(e):
                def body(handle):
                    for waits, fn, inc in lists[e]:
                        for s, v in waits:
                            handle.wait_ge(sem[s], v)
                        if fn is not None:
                            ins = fn(handle)
                            if inc is not None:
                                ins.then_inc(sem[inc[0]], inc[1])
                return body
            for e, dec in (("sp", block.sync), ("pe", block.tensor), ("act", block.scalar),
                           ("dve", block.vector), ("pool", block.gpsimd)):
                if lists[e]:
                    dec(mk(e))


class Kern:
    def __init__(self, stages=4):
        self.stages = stages
        self.nc = bass.Bass("TRN2", target_bir_lowering=False)
        self.es = ExitStack()
        self.P = None
        self.psrr = 0
        self.nps_n = 6
        self.out_toks = []

    def mm(self, out, lhsT, rhs, start=True, stop=True, reads=(), writes=(), sig=True, sgc=False):
        return self.P.op("pe", lambda h: h.matmul(out, lhsT=lhsT, rhs=rhs, start=start, stop=stop,
                                                  skip_group_check=sgc), reads, writes, sig)

    def tr(self, out, in_, ident, reads=(), writes=(), sig=True):
        return self.P.op("pe", lambda h: h.transpose(out=out, in_=in_, identity=ident), reads, writes, sig)

    def act(self, out, in_, func, reads=(), writes=(), scale=None, bias=None, accum_out=None):
        kw = {}
        if scale is not None:
            kw["scale"] = scale
        if bias is not None:
            kw["bias"] = bias
        if accum_out is not None:
            kw["accum_out"] = accum_out
        return self.P.op("act", lambda h: h.activation(out=out, in_=in_, func=func, **kw), reads, writes)

    def cp(self, eng, out, in_, reads=(), writes=()):
        if eng == "act":
            return self.P.op("act", lambda h: h.activation(out=out, in_=in_, func=AF.Copy), reads, writes)
        return self.P.op(eng, lambda h: h.tensor_copy(out=out, in_=in_), reads, writes)

    def tt(self, eng, out, in0, in1, op, reads=(), writes=()):
        return self.P.op(eng, lambda h: h.tensor_tensor(out=out, in0=in0, in1=in1, op=op), reads, writes)

    def ts(self, eng, out, in0, s1, op0, s2=None, op1=None, reads=(), writes=()):
        if op1 is None:
            return self.P.op(eng, lambda h: h.tensor_scalar(out=out, in0=in0, scalar1=s1, scalar2=None, op0=op0),
                             reads, writes)
        return self.P.op(eng, lambda h: h.tensor_scalar(out=out, in0=in0, scalar1=s1, scalar2=s2, op0=op0, op1=op1),
                         reads, writes)

    def stt(self, out, in0, scalar, in1, op0, op1, reads=(), writes=()):
        return self.P.op("dve", lambda h: h.scalar_tensor_tensor(out=out, in0=in0, scalar=scalar, in1=in1,
                                                                 op0=op0, op1=op1), reads, writes)

    def recip(self, out, in_, reads=(), writes=()):
        return self.P.op("dve", lambda h: h.reciprocal(out=out, in_=in_), reads, writes)

    def red(self, out, in_, reads=(), writes=()):
        return self.P.op("dve", lambda h: h.tensor_reduce(out=out, in_=in_, axis=AX.X, op=ALU.add), reads, writes)

    def memset(self, eng, ap, val, writes=()):
        return self.P.op(eng, lambda h: h.memset(ap, val), (), writes)

    def dma(self, sem, out, in_, reads=(), writes=(), eng="sp", is_out=False):
        tok = self.P.dma(sem, lambda h: h.dma_start(out=out, in_=in_), reads, writes, eng)
        if is_out:
            self.out_toks.append(tok)
        return tok

    def nps(self):
        i = self.psrr % self.nps_n
        self.psrr = (i + 1) % self.nps_n
        return i

    def declare(self):
        nc = self.nc
        di = lambda n, s: nc.dram_tensor(n, list(s), F32, kind="ExternalInput").ap()
        do = lambda n, s: nc.dram_tensor(n, list(s), F32, kind="ExternalOutput").ap()
        self.xp = di("xp", (T, D))
        self.xs = di("xs", (NS, D))
        self.ck = di("ck", (2, NS, 2048, 256))
        self.cv = di("cv", (2, NS, 2048, 256))
        self.sa = di("sa", (2, NS, 30, 256))
        self.sc = di("sc", (2, NS, 2, 256))
        self.spl = di("spl", (2, NS, 15, 256))
        self.wpc = di("wpc", (92, 128, 2048))
        self.prm = di("prm", (128, 2 * NPRM))
        self.poolw = di("poolw", (128, 2 * 2 * 128))
        self.fg = di("fg", (128, D))
        self.cst = di("cst", (128, NCST))
        self.yp = do("yp", (T, D))
        self.ys = do("ys", (NS, D))
        self.kp = do("kp", (2, T, 256))
        self.vp = do("vp", (2, T, 256))
        self.apo = do("apo", (2, 30, 256))
        self.cpo = do("cpo", (2, 2, 256))
        self.ppo = do("ppo", (2, 15, 256))
        self.ks = do("ks", (2, NS, 2048, 256))
        self.vs = do("vs", (2, NS, 2048, 256))
        self.aso = do("aso", (2, NS, 30, 256))
        self.cso = do("cso", (2, NS, 2, 256))
        self.pso = do("pso", (2, NS, 15, 256))

    def build_pieces(self):
        pcs = []
        for l in range(2):
            o = 46 * l
            for b in range(4):
                for j in W_IN_ORDER:
                    if j == 4 and b > 0:
                        continue
                    pcs.append(("in", self.wpc[o + j], None))
                if b < 3:
                    pcs.append(("in", self.wpc[o + 4], None))
                for j in range(4):
                    pcs.append(("out", self.wpc[o + 9 + j], None))
            for gi, grp in enumerate(FF_GROUPS):
                for li, p in enumerate(grp):
                    pcs.append(("g", self.wpc[o + 13 + 2 * p], None))
                    pcs.append(("u", self.wpc[o + 14 + 2 * p], None))
                    if li >= 1:
                        pcs.append(("d", self.wpc[o + 35 + grp[li - 1]], (gi % 2, li - 1)))
                pcs.append(("d", self.wpc[o + 35 + grp[-1]], (gi % 2, len(grp) - 1)))
        self.pieces = pcs
        self.pc_issued = 0
        self.pc_cast = 0
        self.pc_next = 0
        self.WD = None

    def _pc_issue(self, j):
        kind, src, _ = self.pieces[j]
        s = j % NSTG
        self.dma(f"wst{s}", self.WSTG[:, s, :], src, writes=[f"wst{s}"])

    def _pc_docast(self, j):
        kind, src, dd = self.pieces[j]
        s = j % NSTG
        if kind == "d":
            g2, li = dd
            out = self.WD[:, g2, 2 * li:2 * li + 2, :]
            in_ = self.WSTG[:, s, :].rearrange("p (c n) -> p c n", c=2)
            wk = [f"wd{g2}.{li}"]
        else:
            r = j % NRING
            out = self.WBF[:, r, :]
            in_ = self.WSTG[:, s, :]
            wk = [f"wbf{r}"]
        self.cp(CAST_ENG, out, in_, reads=[f"wst{s}"], writes=wk)

    def next_piece(self):
        i = self.pc_next
        self.pc_next += 1
        n = len(self.pieces)
        while self.pc_cast < min(n, i + 2):
            j = self.pc_cast
            while self.pc_issued < min(n, j + NSTG):
                self._pc_issue(self.pc_issued)
                self.pc_issued += 1
            if self.pieces[j][0] == "d" and self.WD is None:
                break
            self._pc_docast(j)
            self.pc_cast += 1
        assert self.pc_cast > i, "piece not cast"
        kind, _, dd = self.pieces[i]
        if kind == "d":
            return None, f"wd{dd[0]}.{dd[1]}"
        r = i % NRING
        return self.WBF[:, r, :].rearrange("p (k n) -> p k n", k=8), f"wbf{r}"

    def prm_ap(self, l, col):
        return self.PRM[:, l * NPRM + col:l * NPRM + col + 1]

    def rmsnorm_cols(self, gcol0, l, hbuf, hkey, c0, n, xkeyt, SQ, RS):
        hfn, = hbuf
        pi = self.nps()
        ps = self.PSB[pi]
        for k in range(8):
            s = k % 2
            self.act(SQ[:, s, 0:n], self.XT[:, k, c0:c0 + n], AF.Square, reads=[f"XT.{k}.{xkeyt}"],
                     writes=[f"SQ{s}"])
            self.mm(ps[:, 0:n], self.ONESB[:, :], SQ[:, s, 0:n], start=(k == 0), stop=(k == 7),
                    reads=[f"SQ{s}"], writes=[f"ps{pi}"], sig=True)
        self.act(RS[:, 0:n], ps[:, 0:n], AF.Ln, reads=[f"ps{pi}"], writes=["RS"], scale=1.0 / D, bias=self.EPSC[:, 0:1])
        self.act(RS[:, 0:n], RS[:, 0:n], AF.Exp, reads=["RS"], writes=["RS"], scale=-0.5)
        for k in range(8):
            self.stt(hfn(k), self.XT[:, k, c0:c0 + n], self.prm_ap(l, gcol0 + k), RS[:, 0:n], ALU.mult, ALU.mult,
                     reads=[f"XT.{k}.{xkeyt}", "RS"], writes=[f"{hkey}.{k}"])

    def prologue(self):
        P = self.P
        with ExitStack() as es:
            XS = P.sb("XS", [128, 4, D], F32, es)
            XSS = P.sb("XSS", [NS, D], F32, es)
            STA = P.sb("STA", [30, 2 * NS * 256], F32, es)
            STC = P.sb("STC", [2, 2 * NS * 256], F32, es)
            STP = P.sb("STP", [15, 2 * NS * 256], F32, es)
            PWS = P.sb("PWS", [128, 512], F32, es)
            self.dma("ld_cst", self.CST[:, :], self.cst[:, :], writes=["CST"])
            while self.pc_issued < NSTG:
                self._pc_issue(self.pc_issued)
                self.pc_issued += 1
            self.dma("ld_prm", self.PRM[:, :], self.prm[:, :], writes=["PRM"])
            self.dma("ld_pw", PWS[:, :], self.poolw[:, :], writes=["PWS"])
            self.cp("dve", self.POOLW[:, :], PWS[:, :], reads=["PWS"], writes=["POOLW"])
            self.cp("dve", self.IDB[:, :], self.CST[:, 0:128], reads=["CST"], writes=["IDB"])
            self.cp("dve", self.MSK[:, :], self.CST[:, 384:768], reads=["CST"], writes=["MSK"])
            self.memset("dve", self.ONESB[:, :], 1.0, writes=["ONESB"])
            self.memset("dve", self.EPSC[:, :], EPS, writes=["EPSC"])
            ctoks = [P.lastw[k] for k in ("CST", "PRM", "POOLW", "IDB", "MSK", "ONESB", "EPSC", "ONESF")]
            for e in ("pe", "act", "dve", "pool"):
                P.wait_all(e, ctoks)
            self._prologue_rest(XS, XSS, STA, STC, STP)
            P.drain_dmas()
            P.emit()

    def _cache_copies(self):
        big, small = [], []
        for l in range(2):
            for n in range(NS):
                for src, dst in ((self.ck, self.ks), (self.cv, self.vs)):
                    big.append((dst[l, n, 0:2047, :].rearrange("r c -> (r c)"),
                                src[l, n, 1:2048, :].rearrange("r c -> (r c)")))
                small.append((self.aso[l, n, 0:29, :].rearrange("r c -> (r c)"),
                              self.sa[l, n, 1:30, :].rearrange("r c -> (r c)")))
                small.append((self.cso[l, n, 0:1, :].rearrange("r c -> (r c)"),
                              self.sc[l, n, 1:2, :].rearrange("r c -> (r c)")))
                small.append((self.pso[l, n, 0:14, :].rearrange("r c -> (r c)"),
                              self.spl[l, n, 1:15, :].rearrange("r c -> (r c)")))
        self.copy_q = small + big
        self.issue_copies(len(small))

    def issue_copies(self, n):
        for _ in range(min(n, len(self.copy_q))):
            dst, src = self.copy_q.pop(0)
            self.dma("cpy", dst, src, eng="pool", is_out=True)

    def _prologue_rest(self, XS, XSS, STA, STC, STP):
        P = self.P
        if True:
            for (stg, src, rows, dstT, nm) in ((STA, self.sa, 30, self.SAT, "A"), (STC, self.sc, 2, self.SCT, "C"),
                                               (STP, self.spl, 15, self.SPT, "Pl")):
                self.dma("ld_st" + nm, stg[:, :].rearrange("r (g c) -> r g c", c=256),
                         src.rearrange("l n r c -> r (l n) c"), writes=["ST" + nm])
                pi = self.nps()
                ps = self.PSB[pi]
                i = 0
                for l in range(2):
                    for c in range(2):
                        for n in range(NS):
                            col = (l * NS + n) * 256 + c * 128
                            self.tr(ps[:, i * rows:(i + 1) * rows], stg[0:rows, col:col + 128],
                                    self.CST[0:rows, 0:rows], reads=["ST" + nm, "CST"], writes=[f"ps{pi}"],
                                    sig=(i == 15))
                            i += 1
                w = dstT.shape[-1]
                self.cp("dve", dstT[:, :, :, :, 0:rows].rearrange("p l c n r -> p (l c n) r"),
                        ps[:, 0:16 * rows].rearrange("p (g r) -> p g r", r=rows), reads=[f"ps{pi}"],
                        writes=["S" + nm + "T"])
            self.dma("ld_xs", XSS[:, :], self.xs[:, :], writes=["XSS"])
            pi = self.nps()
            ps = self.PSB[pi]
            for k in range(8):
                self.tr(ps[:, k * NS:(k + 1) * NS], XSS[0:NS, k * 128:(k + 1) * 128], self.CST[0:NS, 0:NS],
                        reads=["XSS", "CST"], writes=[f"ps{pi}"], sig=(k == 7))
            self.cp("dve", self.XT[:, :, T:TT], ps[:, 0:8 * NS].rearrange("p (k n) -> p k n", n=NS),
                    reads=[f"ps{pi}"], writes=[f"XT.{k}.4" for k in range(8)])
            for a in range(16):
                s = a % 4
                self.dma(f"ld_x{s}", XS[:, s, :], self.xp[128 * a:128 * a + 128, :], writes=[f"XS{s}"])
                for kk in range(2):
                    pi = self.nps()
                    ps = self.PSB[pi]
                    for j in range(4):
                        k = 4 * kk + j
                        self.tr(ps[:, j * 128:(j + 1) * 128], XS[:, s, k * 128:(k + 1) * 128], self.CST[:, 0:128],
                                reads=[f"XS{s}", "CST"], writes=[f"ps{pi}"], sig=(j == 3))
                    self.cp("act" if kk else "dve", self.XT[:, 4 * kk:4 * kk + 4, 128 * a:128 * a + 128],
                            ps[:, :].rearrange("p (j t) -> p j t", j=4), reads=[f"ps{pi}"],
                            writes=[f"XT.{k}.{a // 4}" for k in range(4 * kk, 4 * kk + 4)])

    def mixer_layer(self, l):
        P = self.P
        with ExitStack() as es:
            sb = lambda n, s, d: P.sb(f"{n}_{l}", s, d, es)
            Hq = sb("Hq", [128, 8, 516], BF16)
            MIXq = sb("MIXq", [128, 8, 516], BF16)
            QTq = sb("QTq", [128, 4, 512], BF16)
            KT = sb("KT", [128, 2, T], BF16)
            V1 = sb("V1", [128, 8, 256], BF16)
            V2 = sb("V2", [128, 8, 256], BF16)
            V3 = sb("V3", [128, 16, 256], BF16)
            GAq = sb("GAq", [128, 2, 542], F32)
            CPq = sb("CPq", [128, 2, 514], F32)
            PPq = sb("PPq", [128, 2, 527], F32)
            SQ = sb("SQ", [128, 2, 512], BF16)
            RS = sb("RS", [128, 512], F32)
            YA = sb("YA", [128, 2, 512], F32)
            MU = sb("MU", [128, 512], F32)
            RSTD = sb("RSTD", [128, 512], F32)
            T1 = sb("T1", [128, 512], F32)
            PT = sb("PT", [128, 4, 256], BF16)
            KVST = sb("KVST", [128, 4, 256], F32)
            CC = sb("CC", [128, 2, 512], F32)
            SA_ = sb("SA_", [128, 527], F32)
            SB_ = sb("SB_", [128, 527], F32)
            YDP = sb("YDP", [128, 2, 512], BF16)
            QTS = sb("QTS", [128, 2, NS], F32)
            KTS = sb("KTS", [128, 2, NS], F32)
            VTS = sb("VTS", [128, 2, NS], F32)
            CXS = sb("CXS", [128, 2, NS], F32)
            CBS = sb("CBS", [128, 2, NS], F32)
            YAS = sb("YAS", [128, 2, NS], F32)
            SQS = sb("SQS", [128, 2, NS], F32)
            SM1 = sb("SM1", [128, 2, NS, 31], F32)
            SM2 = sb("SM2", [128, 2, NS], F32)
            SM3 = sb("SM3", [128, 2, NS], F32)
            SM4 = sb("SM4", [128, 2, NS], F32)
            SM5 = sb("SM5", [128, 2, NS], F32)
            YDS = sb("YDS", [128, 2, NS], BF16)
            QREP = sb("QREP", [128, 256], F32)
            QBC = sb("QBC", [128, 2, 128], F32)
            KSB = sb("KSB", [128, 3, 256], F32)
            VSB = sb("VSB", [128, 3, 256], F32)
            PRD = sb("PRD", [128, 256], F32)
            SCS = sb("SCS", [128, 3, 4], F32)
            PEX = sb("PEX", [128, 3, 4], F32)
            UZ = sb("UZ", [128, 3, 4], F32)
            self.nps_n = 8
            if l == 0:
                self._cache_copies()
            self._mixer_body(l, locals())
            P.drain_dmas()
            P.emit()

    def _mixer_body(self, l, B):
        P = self.P
        Hq, MIXq, QTq, KT, V1, V2, V3 = B["Hq"], B["MIXq"], B["QTq"], B["KT"], B["V1"], B["V2"], B["V3"]
        GAq, CPq, PPq, SQ, RS, YA, MU, RSTD, T1 = (B[k] for k in ("GAq", "CPq", "PPq", "SQ", "RS", "YA",
                                                                "MU", "RSTD", "T1"))
        PT, KVST, CC, SA_, SB_, YDP = (B[k] for k in ("PT", "KVST", "CC", "SA_", "SB_", "YDP"))
        SIG = T1
        QTS, KTS, VTS, CXS, CBS, YAS, SQS = (B[k] for k in ("QTS", "KTS", "VTS", "CXS", "CBS", "YAS", "SQS"))
        PSB = self.PSB
        self.memset("dve", GAq[:, :, 0:30], 0.0, writes=["GAq.pre"])
        self.memset("dve", CPq[:, :, 0:2], 0.0, writes=["CPq.pre"])
        self.memset("dve", PPq[:, :, 0:15], 0.0, writes=["PPq.pre"])
        self.memset("pool", QTq[:, :, :], 0.0, writes=[f"QTq.{h_}" for h_ in range(4)])
        self.memset("pool", KT[:, :, :], 0.0, writes=["KT.i"])
        self.memset("pool", V3[:, :, :], 0.0, writes=["V3.i"])
        kvst_i = 0
        import os
        dbg = os.environ.get("KDBG", "9.9").split(".")
        dbg_b, dbg_l = int(dbg[0]), int(dbg[1])
        dbg_np = int(dbg[2]) if len(dbg) > 2 else 99
        def kv_tm(b, j, W, wkey):
            nonlocal kvst_i
            smp = (b == 3)
            c0 = 512 * b
            dst = self.kp if j == 3 else self.vp
            for a in range(4):
                pi = self.nps()
                for k in range(8):
                    self.mm(PSB[pi][:, 0:256], Hq[:, k, 128 * a:128 * a + 128], W[:, k, :], start=(k == 0),
                            stop=(k == 7), reads=[wkey] + hreads, writes=[f"ps{pi}"], sig=(k == 7))
                s = kvst_i % 4
                kvst_i += 1
                SK = os.environ.get("KSKIP", "")
                self.cp("act", KVST[:, s, :], PSB[pi][:, 0:256], reads=[f"ps{pi}"], writes=[f"KVST{s}"])
                if "vpdma" not in SK:
                  tok = self.dma(f"st_kv{s}", dst[l, c0 + 128 * a:c0 + 128 * a + 128, :], KVST[:, s, :],
                               reads=[f"KVST{s}"], writes=([f"vp.{b}.{a}"] if j == 4 else []), is_out=True, eng=AUXQ)
                if j == 4 and "v1" not in SK:
                    t1 = (4 * b + a) % 8
                    self.cp("dve", V1[:, t1, :], KVST[:, s, :], reads=[f"KVST{s}"], writes=[f"V1.{t1}"])
            if smp:
                pi = self.nps()
                for k in range(8):
                    self.mm(PSB[pi][0:NS, 0:256], Hq[:, k, 512:516], W[:, k, :], start=(k == 0),
                            stop=(k == 7), reads=[wkey] + hsreads, writes=[f"ps{pi}"], sig=(k == 7))
                ri = 0 if j == 3 else 1
                rb_, rk_ = (B["QREP"], "QREP") if j == 3 else (B["PRD"], "PRD")
                self.cp("act", rb_[0:NS, :], PSB[pi][0:NS, 0:256], reads=[f"ps{pi}"], writes=[rk_])
                self.dma(f"st_row{ri}", (self.ks if j == 3 else self.vs)[l, :, 2047, :], rb_[0:NS, :],
                         reads=[rk_], is_out=True, eng=AUXQ)
            if j == 4 and not os.environ.get("KSKIP_RB"):
                vsrc = self.vp[l, c0:c0 + 512, :]
                vrd = [f"vp.{b}.{a_}" for a_ in range(4)]
                s0 = (4 * b) % 8
                self.dma("ld_v2", V2[:, s0:s0 + 4, :], vsrc.rearrange("(p r) e -> p r e", r=4), reads=vrd,
                         writes=[f"V2.{s0 + r}" for r in range(4)], eng="pool")
                self.dma("ld_v3", V3[32 * b:32 * b + 32, :, :], vsrc.rearrange("(i r) e -> i r e", r=16),
                         reads=vrd + ["V3.i"], writes=[f"V3.{b}"], eng="pool")
                if smp:
                    for c in range(2):
                        pi = self.nps()
                        for k in range(8):
                            self.mm(PSB[pi][:, 0:NS], W[:, k, c * 128:(c + 1) * 128], Hq[:, k, 512:516],
                                    start=(k == 0), stop=(k == 7), reads=[wkey] + hsreads,
                                    writes=[f"ps{pi}"], sig=(k == 7))
                        self.cp("dve", VTS[:, c, :], PSB[pi][:, 0:NS], reads=[f"ps{pi}"], writes=["VTS"])


        for b in range(4):
            if b > dbg_b:
                break
            lvl = dbg_l if b == dbg_b else 9
            self.issue_copies(1)
            smp = (b == 3)
            ncol = 516 if smp else 512
            c0 = 512 * b
            def emit_norm1(bb):
                self.rmsnorm_cols(76, l, (lambda k: Hq[:, k, 0:512],), "Hq", 512 * bb, 512, bb, SQ, RS)
                if bb == 3:
                    self.rmsnorm_cols(76, l, (lambda k: Hq[:, k, 512:516],), "HqS", T, NS, 4, SQ, RS)
            if b == 0:
                emit_norm1(b)
            hreads = [f"Hq.{k}" for k in range(8)]
            hsreads = [f"HqS.{k}" for k in range(8)]
            if lvl < 2:
                break

            def proj_fm(W, wkey, c, evac, evac_s=None):
                pi = self.nps()
                for k in range(8):
                    self.mm(PSB[pi][:, 0:512], W[:, k, c * 128:(c + 1) * 128], Hq[:, k, 0:512], start=(k == 0),
                            stop=(k == 7), reads=[wkey] + hreads, writes=[f"ps{pi}"], sig=(k == 7))
                evac(PSB[pi][:, 0:512], f"ps{pi}")
                if smp and evac_s is not None:
                    pi = self.nps()
                    for k in range(8):
                        self.mm(PSB[pi][:, 0:NS], W[:, k, c * 128:(c + 1) * 128], Hq[:, k, 512:516], start=(k == 0),
                                stop=(k == 7), reads=[wkey] + hsreads, writes=[f"ps{pi}"], sig=(k == 7))
                    evac_s(PSB[pi][:, 0:NS], f"ps{pi}")

            for jn, j in enumerate(W_IN_ORDER):
                if j == 4 and b > 0:
                    continue
                W, wkey = self.next_piece()
                if j in (3, 4):
                    kv_tm(b, j, W, wkey)
                    if j == 3:
                        for c in range(2):
                            proj_fm(W, wkey, c,
                                    (lambda ps, pk, c=c: self.cp("act", KT[:, c, c0:c0 + 512], ps, reads=[pk, "KT.i"],
                                                                 writes=[f"KT.{b}"])),
                                    (lambda ps, pk, c=c: self.cp("dve", KTS[:, c, :], ps, reads=[pk], writes=["KTS"])))
                    continue
                for c in range(2):
                    if j == 2:
                        proj_fm(W, wkey, c,
                                (lambda ps, pk, c=c: (self.cp("act", QTq[0:64, 2 * c, :], ps[0:64, :], reads=[pk],
                                                              writes=[f"QTq.{2 * c}"]),
                                                      self.cp("act", QTq[64:128, 2 * c + 1, :], ps[64:128, :], reads=[pk],
                                                              writes=[f"QTq.{2 * c + 1}"]))),
                                (lambda ps, pk, c=c: self.cp("dve", QTS[:, c, :], ps, reads=[pk], writes=["QTS"])))
                    elif j == 0:
                        proj_fm(W, wkey, c,
                                (lambda ps, pk, c=c: self.cp("act", GAq[:, c, 30:542], ps, reads=[pk],
                                                             writes=[f"GAq.{c}"])),
                                (lambda ps, pk, c=c: self.cp("dve", self.SAT[:, l, c, :, 30], ps, reads=[pk],
                                                             writes=["SAT.new"])))
                    elif j == 1:
                        def ev(ps, pk, c=c):
                            self.act(SIG[:, :], ps, AF.Sigmoid, reads=[pk], writes=["T1"])
                            self.tt("dve", GAq[:, c, 30:542], GAq[:, c, 30:542], SIG[:, :], ALU.mult,
                                    reads=["T1", f"GAq.{c}"], writes=[f"GAq.{c}"])

                        def evs(ps, pk, c=c):
                            self.act(CBS[:, c, :], ps, AF.Sigmoid, reads=[pk], writes=["CBS"])
                            self.tt("dve", self.SAT[:, l, c, :, 30], self.SAT[:, l, c, :, 30], CBS[:, c, :], ALU.mult,
                                    reads=["CBS", "SAT.new"], writes=["SAT.new"])
                        proj_fm(W, wkey, c, ev, evs)
                    elif j == 5:
                        proj_fm(W, wkey, c,
                                (lambda ps, pk, c=c: self.cp("act", CPq[:, c, 2:514], ps, reads=[pk],
                                                             writes=[f"CPq.{c}"])),
                                (lambda ps, pk, c=c: self.cp("dve", CXS[:, c, :], ps, reads=[pk], writes=["CXS"])))
                    elif j == 7:
                        def ev(ps, pk, c=c):
                            self.tt("dve", CPq[:, c, 2:514], ps, CPq[:, c, 2:514], ALU.mult, reads=[pk, f"CPq.{c}"],
                                    writes=[f"CPq.{c}"])
                            self.ts("dve", CC[:, c, :], CPq[:, c, 0:512], self.prm_ap(l, 68 + 3 * c), ALU.mult,
                                    reads=[f"CPq.{c}", "CPq.pre"], writes=[f"CC.{c}"])
                            for k in (1, 2):
                                self.stt(CC[:, c, :], CPq[:, c, k:k + 512], self.prm_ap(l, 68 + 3 * c + k), CC[:, c, :],
                                         ALU.mult, ALU.add, reads=[f"CPq.{c}", "CPq.pre", f"CC.{c}"],
                                         writes=[f"CC.{c}"])

                        def evs(ps, pk, c=c):
                            self.tt("dve", self.SCT[:, l, c, :, 2], ps, CXS[:, c, :], ALU.mult, reads=[pk, "CXS"],
                                    writes=["SCT.new"])
                        proj_fm(W, wkey, c, ev, evs)
                    elif j == 6:
                        def evs(ps, pk, c=c):
                            self.tt("dve", B["SM1"][:, c, :, 0:3], self.SCT[:, l, c, :, :],
                                    self.PRM[:, l * NPRM + 68 + 3 * c:l * NPRM + 71 + 3 * c].unsqueeze(1).to_broadcast(
                                        [128, NS, 3]), ALU.mult, reads=["SCT.new", "SCT"], writes=["SM1"])
                            self.red(B["SM2"][:, c, :], B["SM1"][:, c, :, 0:3], reads=["SM1"], writes=["SM2"])
                            self.tt("dve", MIXq[:, 4 + c, 512:516], ps, B["SM2"][:, c, :], ALU.mult,
                                    reads=[pk, "SM2"], writes=[f"MIXS.{4 + c}"])
                        proj_fm(W, wkey, c,
                                (lambda ps, pk, c=c: self.tt("dve", MIXq[:, 4 + c, 0:512], ps, CC[:, c, :], ALU.mult,
                                                             reads=[pk, f"CC.{c}"], writes=[f"MIXq.{4 + c}"])),
                                evs)
                    elif j == 8:
                        proj_fm(W, wkey, c,
                                (lambda ps, pk, c=c: self.cp("act", PPq[:, c, 15:527], ps, reads=[pk],
                                                             writes=[f"PPq.{c}"])),
                                (lambda ps, pk, c=c: self.cp("dve", self.SPT[:, l, c, :, 15], ps, reads=[pk],
                                                             writes=["SPT.new"])))
                if j == 8:
                    self._mixer_D(l, b, B, part=0)
            if lvl < 3:
                break
            if b < 3:
                emit_norm1(b + 1)
            conv_ops = self._convA_ops(l, B)
            self._mixer_D(l, b, B, part=1)
            self.nps_n = 4
            ptc = [0]
            nq_ = (len(conv_ops) + 3) // 4
            if smp:
                self._sample_mix(l, B, part="pre")
            for h in range(4):
                self._attention_head(l, b, B, h, ptc, conv_ops[h * nq_:(h + 1) * nq_])
                if smp:
                    self._sample_mix(l, B, part=h)
            self.nps_n = 8
            self._mixer_A_ln(l, B)
            if b < 3:
                Wv, wvk = self.next_piece()
                kv_tm(b + 1, 4, Wv, wvk)
            if lvl < 4:
                break
            if smp:
                self._state_outputs(l, B)
            if b < 3:
                self.cp("dve", GAq[:, :, 0:30], GAq[:, :, 512:542], reads=["GAq.0", "GAq.1", "GAq.pre"],
                        writes=["GAq.pre"])
                self.cp("dve", CPq[:, :, 0:2], CPq[:, :, 512:514], reads=["CPq.0", "CPq.1", "CPq.pre"],
                        writes=["CPq.pre"])
                self.cp("dve", PPq[:, :, 0:15], PPq[:, :, 512:527], reads=["PPq.0", "PPq.1", "PPq.pre"],
                        writes=["PPq.pre"])
            mreads = [f"MIXq.{k}" for k in range(8)]
            msreads = [f"MIXS.{k}" for k in range(8)]
            for j in range(4):
                W, wkey = self.next_piece()
                for c in range(2):
                    oc = 2 * j + c
                    pi = self.nps()
                    for k in range(8):
                        self.mm(PSB[pi][:, 0:512], W[:, k, c * 128:(c + 1) * 128], MIXq[:, k, 0:512], start=(k == 0),
                                stop=(k == 7), reads=[wkey] + mreads, writes=[f"ps{pi}"], sig=(k == 7))
                    self.tt("dve", self.XT[:, oc, c0:c0 + 512], PSB[pi][:, 0:512], self.XT[:, oc, c0:c0 + 512], ALU.add,
                            reads=[f"ps{pi}", f"XT.{oc}.{b}"], writes=[f"XT.{oc}.{b}"])
                    if smp:
                        pi = self.nps()
                        for k in range(8):
                            self.mm(PSB[pi][:, 0:NS], W[:, k, c * 128:(c + 1) * 128], MIXq[:, k, 512:516],
                                    start=(k == 0), stop=(k == 7), reads=[wkey] + msreads, writes=[f"ps{pi}"],
                                    sig=(k == 7))
                        self.tt("dve", self.XT[:, oc, T:TT], PSB[pi][:, 0:NS], self.XT[:, oc, T:TT], ALU.add,
                                reads=[f"ps{pi}", f"XT.{oc}.4"], writes=[f"XT.{oc}.4"])

    def _ln_silu(self, l, n, ya_fn, sq_fn, out_fn, yakeys, outkeys, MU, RSTD, T1, sqkeys=("CC.0", "CC.1")):
        PSB = self.PSB
        p1 = self.nps()
        p2 = self.nps()
        for c in range(2):
            self.act(sq_fn(c), ya_fn(c), AF.Square, reads=[yakeys[c]], writes=[sqkeys[c]])
        for c in range(2):
            self.mm(PSB[p1][:, 0:n], self.CST[:, 256:384], ya_fn(c), start=(c == 0), stop=(c == 1),
                    reads=[yakeys[c], "CST"], writes=[f"ps{p1}"], sig=(c == 1))
        for c in range(2):
            self.mm(PSB[p2][:, 0:n], self.CST[:, 256:384], sq_fn(c), start=(c == 0), stop=(c == 1),
                    reads=[sqkeys[c], "CST"], writes=[f"ps{p2}"], sig=(c == 1))
        self.cp("act", MU[:, 0:n], PSB[p1][:, 0:n], reads=[f"ps{p1}"], writes=["MU"])
        self.tt("dve", RSTD[:, 0:n], MU[:, 0:n], MU[:, 0:n], ALU.mult, reads=["MU"], writes=["RSTD"])
        self.tt("dve", RSTD[:, 0:n], PSB[p2][:, 0:n], RSTD[:, 0:n], ALU.subtract, reads=[f"ps{p2}", "RSTD"],
                writes=["RSTD"])
        self.ts("dve", RSTD[:, 0:n], RSTD[:, 0:n], 0.0, ALU.max, reads=["RSTD"], writes=["RSTD"])
        self.act(RSTD[:, 0:n], RSTD[:, 0:n], AF.Ln, reads=["RSTD"], writes=["RSTD"], bias=self.EPSC[:, 0:1])
        self.act(RSTD[:, 0:n], RSTD[:, 0:n], AF.Exp, reads=["RSTD"], writes=["RSTD"], scale=-0.5)
        for c in range(2):
            self.tt("dve", T1[:, 0:n], ya_fn(c), MU[:, 0:n], ALU.subtract, reads=[yakeys[c], "MU"], writes=["T1"])
            self.tt("dve", T1[:, 0:n], T1[:, 0:n], RSTD[:, 0:n], ALU.mult, reads=["T1", "RSTD"], writes=["T1"])
            self.act(out_fn(c), T1[:, 0:n], AF.Silu, reads=["T1"], writes=[outkeys[c]],
                     scale=self.prm_ap(l, 64 + c), bias=self.prm_ap(l, 66 + c))

    def _convA_ops(self, l, B):
        GAq, YA = B["GAq"], B["YA"]
        ops = []
        for c in range(2):
            rd = [f"GAq.{c}", "GAq.pre"]
            ops.append(lambda c=c, rd=rd: self.ts("dve", YA[:, c, :], GAq[:, c, 0:512], self.prm_ap(l, 31 * c), ALU.mult,
                                                  s2=self.prm_ap(l, 62 + c), op1=ALU.add, reads=rd, writes=[f"YA.{c}"]))
            for k in range(1, 31):
                ops.append(lambda c=c, rd=rd, k=k: self.stt(YA[:, c, :], GAq[:, c, k:k + 512], self.prm_ap(l, 31 * c + k),
                                                            YA[:, c, :], ALU.mult, ALU.add, reads=rd + [f"YA.{c}"],
                                                            writes=[f"YA.{c}"]))
        return ops

    def _mixer_A_ln(self, l, B):
        YA, SQF, MIXq = B["YA"], B["CC"], B["MIXq"]
        self._ln_silu(l, 512, lambda c: YA[:, c, :], lambda c: SQF[:, c, :], lambda c: MIXq[:, c, 0:512],
                      ["YA.0", "YA.1"], ["MIXq.0", "MIXq.1"], B["MU"], B["RSTD"], B["T1"])

    def _mixer_D(self, l, b, B, part):
        PPq, SA_, SB_, YDP, MIXq = B["PPq"], B["SA_"], B["SB_"], B["YDP"], B["MIXq"]
        PSB = self.PSB
        if part == 1:
            for c in range(2):
                pi = self.nps()
                self.mm(PSB[pi][:, 0:512], self.POOLW[:, (2 * l + c) * 128:(2 * l + c + 1) * 128], YDP[:, c, :],
                        reads=[f"YDP.{c}", "POOLW"], writes=[f"ps{pi}"])
                self.ts("dve", MIXq[:, 6 + c, 0:512], PSB[pi][:, 0:512], self.prm_ap(l, 74 + c), ALU.mult,
                        reads=[f"ps{pi}"], writes=[f"MIXq.{6 + c}"])
            return
        for c in range(2):
            rd = [f"PPq.{c}", "PPq.pre"]
            pp = PPq[:, c, :]
            self.tt("dve", SA_[:, 1:527], pp[:, 1:527], pp[:, 0:526], ALU.add, reads=rd, writes=["SA_"])
            self.tt("dve", SB_[:, 3:527], SA_[:, 3:527], SA_[:, 1:525], ALU.add, reads=["SA_"], writes=["SB_"])
            if c == 0:
                lo, hi, wlo, whi = SA_, SB_, 2.0, 4.0
                keys = ["SA_", "SB_"]
            else:
                self.tt("dve", SA_[:, 7:527], SB_[:, 7:527], SB_[:, 3:523], ALU.add, reads=["SB_", "SA_"], writes=["SA_"])
                self.tt("dve", SB_[:, 15:527], SA_[:, 15:527], SA_[:, 7:519], ALU.add, reads=["SA_", "SB_"],
                        writes=["SB_"])
                lo, hi, wlo, whi = SA_, SB_, 8.0, 16.0
                keys = ["SA_", "SB_"]
            self.stt(YDP[0:64, c, :], lo[0:64, 15:527], 1.0 / wlo, pp[0:64, 15:527], ALU.mult, ALU.subtract,
                     reads=keys + rd, writes=[f"YDP.{c}"])
            self.stt(YDP[64:128, c, :], hi[64:128, 15:527], 1.0 / whi, pp[64:128, 15:527], ALU.mult, ALU.subtract,
                     reads=keys + rd, writes=[f"YDP.{c}"])
            if b == 0:
                for (buf, p0) in ((lo, 0), (hi, 64)):
                    self.tt("dve", B["T1"][p0:p0 + 64, 0:16], buf[p0:p0 + 64, 15:31],
                            self.CST[p0:p0 + 64, 768 + 16 * c:784 + 16 * c], ALU.mult, reads=keys + ["CST"],
                            writes=["T1"])
                    self.tt("dve", YDP[p0:p0 + 64, c, 0:16], B["T1"][p0:p0 + 64, 0:16], pp[p0:p0 + 64, 15:31],
                            ALU.subtract, reads=["T1"] + rd, writes=[f"YDP.{c}"])

    def _attention_head(self, l, b, B, h, ptc, mid_ops):
        QTq, KT, V1, V2, V3, PT, RCP, MIXq = (B[k] for k in ("QTq", "KT", "V1", "V2", "V3", "PT", "T1", "MIXq"))
        PSB = self.PSB
        MSK = self.MSK
        c0 = 512 * b
        bu, bz = (4, 5) if h % 2 == 0 else (6, 7)
        if True:
            c = h // 2
            pb = (h % 2) * 64
            jobs = []
            for qq in range(4):
                qb = 4 * b + qq
                kt = []
                if qb > 0:
                    kt.append((slice(128 * (qb - 1), 128 * qb), V1[:, (qb - 1) % 8, :], f"V1.{(qb - 1) % 8}",
                               f"KT.{(qb - 1) // 4}"))
                kt.append((slice(128 * qb, 128 * qb + 128), V1[:, qb % 8, :], f"V1.{qb % 8}", f"KT.{b}"))
                jobs.append((kt, 128, slice(128 * qq, 128 * qq + 128), 128,
                             MSK[:, 0:256] if qb > 0 else MSK[:, 128:256], None))
            for r4 in range(4):
                kt = []
                if b > 0:
                    s = (4 * (b - 1) + r4) % 8
                    kt.append((slice(512 * (b - 1) + r4, 512 * b, 4), V2[:, s, :], f"V2.{s}", f"KT.{b - 1}"))
                s = (4 * b + r4) % 8
                kt.append((slice(512 * b + r4, 512 * b + 512, 4), V2[:, s, :], f"V2.{s}", f"KT.{b}"))
                jobs.append((kt, 128, slice(r4, 512, 4), 128, MSK[:, 0:256] if b > 0 else MSK[:, 128:256], None))
            for g4 in range(4):
                kt = [(slice(r16, T, 16), V3[:, r16, :], None, None) for r16 in range(4 * g4, 4 * g4 + 4)]
                qsl = [slice(r16, 512, 16) for r16 in range(4 * g4, 4 * g4 + 4)]
                jobs.append((kt, 128, qsl, 32,
                             MSK[:, 256 + 32 * b:256 + 32 * b + 32].unsqueeze(1).to_broadcast([128, 4, 32]), g4))
            ktall = [f"KT.{i}" for i in range(b + 1)] + ["KT.i"]
            v3all = [f"V3.{i}" for i in range(b + 1)] + ["V3.i"]
            state = {"first": True}

            def qk(job):
                kt, kp, qs, nq, msk, g4 = job
                pi = self.nps()
                ncols = nq * len(kt)
                mout = PSB[pi][0:kp, 0:ncols]
                if g4 is not None:
                    mout = mout.rearrange("p (j i) -> p j i", j=4)
                self.mm(mout, self.IDB[0:kp, 0:kp], msk, start=True, stop=False,
                        reads=["IDB", "MSK"], writes=[f"ps{pi}"], sig=False)
                for i, (ks, vt, vk, kk) in enumerate(kt):
                    self.mm(PSB[pi][0:kp, i * nq:(i + 1) * nq], KT[:, c, ks], QTq[:, h, qs[i] if g4 is not None else qs],
                            start=False, stop=(i == len(kt) - 1), reads=[f"QTq.{h}"] + ([kk, "KT.i"] if kk else ktall),
                            writes=[f"ps{pi}"], sig=(i == len(kt) - 1))
                s = ptc[0] % 4
                ptc[0] += 1
                self.act(PT[0:kp, s, 0:ncols], PSB[pi][0:kp, 0:ncols], AF.Exp, reads=[f"ps{pi}"], writes=[f"PT{s}"],
                         scale=0.125)
                return s

            def pv(job, s):
                kt, kp, qs, nq, msk, g4 = job
                for i, (ks, vt, vk, kk) in enumerate(kt):
                    qo = qs[i] if g4 is not None else qs
                    self.mm(PSB[bu][0:64, qo], vt[0:kp, h * 64:(h + 1) * 64] if vk else vt[:, h * 64:(h + 1) * 64],
                            PT[0:kp, s, i * nq:(i + 1) * nq], start=state["first"], stop=True,
                            reads=[f"PT{s}"] + ([vk] if vk else v3all), writes=[f"ps{bu}"], sig=False, sgc=True)
                    self.mm(PSB[bz][0:64, qo], self.ONESB[0:kp, 0:64], PT[0:kp, s, i * nq:(i + 1) * nq],
                            start=state["first"], stop=True, reads=[f"PT{s}", "ONESB"], writes=[f"ps{bz}"],
                            sig=(i == len(kt) - 1), sgc=True)
                    state["first"] = False

            prev = None
            for job in jobs:
                s = qk(job)
                if prev is not None:
                    pv(*prev)
                prev = (job, s)
            pv(*prev)
            for f_ in mid_ops:
                f_()
            self.act(RCP[0:64, :], PSB[bz][0:64, :], AF.Ln, reads=[f"ps{bz}"], writes=["T1"])
            self.act(RCP[0:64, :], RCP[0:64, :], AF.Exp, reads=["T1"], writes=["T1"], scale=-1.0)
            self.tt("dve", MIXq[pb:pb + 64, 2 + c, 0:512], PSB[bu][0:64, :], RCP[0:64, :], ALU.mult,
                    reads=[f"ps{bu}", "T1"], writes=[f"MIXq.{2 + c}"])

    def _sample_mix(self, l, B, part):
        PSB = self.PSB
        MIXq, YAS, SQS, SM1, SM2, SM3, SM4, SM5, YDS = (B[k] for k in ("MIXq", "YAS", "SQS", "SM1", "SM2", "SM3", "SM4",
                                                                     "SM5", "YDS"))
        QTS, KTS, VTS, QREP, QBC, KSB, VSB, PRD, SCS, PEX, UZ = (B[k] for k in ("QTS", "KTS", "VTS", "QREP", "QBC", "KSB",
                                                                             "VSB", "PRD", "SCS", "PEX", "UZ"))
        if part == "pre":
            self._sample_pre(l, B)
        else:
            self._sample_seq(l, B, part)

    def _sample_pre(self, l, B):
        PSB = self.PSB
        MIXq, YAS, SQS, SM1, SM2, SM3, SM4, SM5, YDS = (B[k] for k in ("MIXq", "YAS", "SQS", "SM1", "SM2", "SM3", "SM4",
                                                                     "SM5", "YDS"))
        QTS, KTS, VTS, QREP, QBC, KSB, VSB, PRD, SCS, PEX, UZ = (B[k] for k in ("QTS", "KTS", "VTS", "QREP", "QBC", "KSB",
                                                                             "VSB", "PRD", "SCS", "PEX", "UZ"))
        for c in range(2):
            self.tt("dve", SM1[:, c, :, :], self.SAT[:, l, c, :, :],
                    self.PRM[:, l * NPRM + 31 * c:l * NPRM + 31 * c + 31].unsqueeze(1).to_broadcast([128, NS, 31]),
                    ALU.mult, reads=["SAT.new", "SAT", "SAT"], writes=["SM1"])
            self.red(YAS[:, c, :], SM1[:, c, :, :], reads=["SM1"], writes=["YAS"])
            self.ts("dve", YAS[:, c, :], YAS[:, c, :], self.prm_ap(l, 62 + c), ALU.add, reads=["YAS"], writes=["YAS"])
        self._ln_silu(l, NS, lambda c: YAS[:, c, :], lambda c: SQS[:, c, :], lambda c: MIXq[:, c, 512:516],
                      ["YAS", "YAS"], ["MIXS.0", "MIXS.1"], B["MU"], B["RSTD"], B["T1"], sqkeys=("SQS", "SQS"))
        for c in range(2):
            for (p0, w) in ((0, (2, 8)[c]), (64, (4, 16)[c])):
                self.red(SM2[p0:p0 + 64, c, :], self.SPT[p0:p0 + 64, l, c, :, 16 - w:16], reads=["SPT.new", "SPT"],
                         writes=["SM2"])
                self.stt(YDS[p0:p0 + 64, c, :], SM2[p0:p0 + 64, c, :], 1.0 / w, self.SPT[p0:p0 + 64, l, c, :, 15],
                         ALU.mult, ALU.subtract, reads=["SM2", "SPT.new"], writes=["YDS"])
            pi = self.nps()
            self.mm(PSB[pi][:, 0:NS], self.POOLW[:, (2 * l + c) * 128:(2 * l + c + 1) * 128], YDS[:, c, :],
                    reads=["YDS", "POOLW"], writes=[f"ps{pi}"])
            self.ts("dve", MIXq[:, 6 + c, 512:516], PSB[pi][:, 0:NS], self.prm_ap(l, 74 + c), ALU.mult,
                    reads=[f"ps{pi}"], writes=[f"MIXS.{6 + c}"])
        self.tt("dve", SM3[:, :, :], QTS[:, :, :], KTS[:, :, :], ALU.mult, reads=["QTS", "KTS"], writes=["SM3"])
        pi = self.nps()
        for c in range(2):
            self.mm(PSB[pi][:, c * NS:(c + 1) * NS], self.CST[:, 128:256], SM3[:, c, :], reads=["SM3", "CST"],
                    writes=[f"ps{pi}"], sig=(c == 1))
        self.act(SM4[:, :, :], PSB[pi][:, 0:2 * NS].rearrange("p (c n) -> p c n", n=NS), AF.Exp, reads=[f"ps{pi}"],
                 writes=["SM4"], scale=0.125)
        self.ts("dve", SM4[:, :, :], SM4[:, :, :], 3.0, ALU.mult, reads=["SM4"], writes=["SM4"])
        self.tt("dve", SM5[:, :, :], SM4[:, :, :], VTS[:, :, :], ALU.mult, reads=["SM4", "VTS"], writes=["SM5"])

    def _sample_seq(self, l, B, n):
        PSB = self.PSB
        MIXq, YAS, SQS, SM1, SM2, SM3, SM4, SM5, YDS = (B[k] for k in ("MIXq", "YAS", "SQS", "SM1", "SM2", "SM3", "SM4",
                                                                     "SM5", "YDS"))
        QTS, KTS, VTS, QREP, QBC, KSB, VSB, PRD, SCS, PEX, UZ = (B[k] for k in ("QTS", "KTS", "VTS", "QREP", "QBC", "KSB",
                                                                             "VSB", "PRD", "SCS", "PEX", "UZ"))
        if True:
            for c in range(2):
                self.cp("dve", QBC[:, c, :], QTS[:, c, n:n + 1].to_broadcast([128, 128]), reads=["QTS"], writes=["QBC"])
            pi = self.nps()
            for c in range(2):
                self.mm(PSB[pi][:, c * 128:(c + 1) * 128], QBC[:, c, :], self.CST[:, 0:128], reads=["QBC", "CST"],
                        writes=[f"ps{pi}"], sig=(c == 1))
            self.cp("act", QREP[:, :], PSB[pi][:, 0:256], reads=[f"ps{pi}"], writes=["QREP"])
            for ci, d in enumerate((1, 4, 16)):
                r0 = 2048 - 128 * d
                self.dma(f"ld_ks{ci}", KSB[:, ci, :], self.ck[l, n, r0:2048:d, :], writes=[f"KSB{ci}"], eng=AUXQ)
                self.dma(f"ld_vs{ci}", VSB[:, ci, :], self.cv[l, n, r0:2048:d, :], writes=[f"VSB{ci}"], eng=AUXQ)
                self.tt("dve", PRD[:, :], KSB[:, ci, :], QREP[:, :], ALU.mult, reads=[f"KSB{ci}", "QREP"],
                        writes=["PRD"])
                self.red(SCS[:, ci, :], PRD[:, :].rearrange("p (h e) -> p h e", e=64), reads=["PRD"],
                         writes=[f"SCS{ci}"])
                self.act(PEX[:, ci, :], SCS[:, ci, :], AF.Exp, reads=[f"SCS{ci}"], writes=[f"PEX{ci}"], scale=0.125)
            pi = self.nps()
            ps = PSB[pi]
            for g in range(3):
                for ci in range(3):
                    lhs = self.ONESF[:, :] if g == 2 else VSB[:, ci, g * 128:(g + 1) * 128]
                    self.mm(ps[:, 4 * g:4 * g + 4], lhs, PEX[:, ci, :], start=(g == 0 and ci == 0), stop=(ci == 2),
                            reads=[f"VSB{ci}", f"PEX{ci}", "ONESF"], writes=[f"ps{pi}"], sig=(g == 2 and ci == 2),
                            sgc=True)
            self.cp("act", UZ[:, :, :], ps[:, 0:12].rearrange("p (g h) -> p g h", h=4), reads=[f"ps{pi}"],
                    writes=["UZ"])
            for c in range(2):
                for hh in range(2):
                    h = 2 * c + hh
                    p0 = 64 * hh
                    self.tt("dve", SM2[p0:p0 + 64, c, 0:1], UZ[p0:p0 + 64, c, h:h + 1], SM5[p0:p0 + 64, c, n:n + 1],
                            ALU.add, reads=["UZ", "SM5"], writes=["SM2"])
                    self.tt("dve", SM2[p0:p0 + 64, c, 1:2], UZ[p0:p0 + 64, 2, h:h + 1], SM4[p0:p0 + 64, c, n:n + 1],
                            ALU.add, reads=["UZ", "SM4"], writes=["SM2"])
                    self.recip(SM2[p0:p0 + 64, c, 1:2], SM2[p0:p0 + 64, c, 1:2], reads=["SM2"], writes=["SM2"])
                    self.tt("dve", MIXq[p0:p0 + 64, 2 + c, 512 + n:513 + n], SM2[p0:p0 + 64, c, 0:1],
                            SM2[p0:p0 + 64, c, 1:2], ALU.mult, reads=["SM2"], writes=[f"MIXS.{2 + c}"])

    def _state_outputs(self, l, B):
        PSB = self.PSB
        YA = B["YA"]
        STOB = ((B["SA_"], "SA_"), (B["SB_"], "SB_"), (B["MU"], "MU"))
        GAq, CPq, PPq = B["GAq"], B["CPq"], B["PPq"]
        for si, (buf, off, rows, dst, key) in enumerate(((GAq, 30, 30, self.apo, "GAq"), (CPq, 2, 2, self.cpo, "CPq"),
                                                         (PPq, 15, 15, self.ppo, "PPq"))):
            pi = self.nps()
            for c in range(2):
                self.tr(PSB[pi][0:rows, c * 128:(c + 1) * 128], buf[:, c, off + 512 - rows:off + 512],
                        self.CST[:, 0:128], reads=[f"{key}.{c}", "CST"], writes=[f"ps{pi}"], sig=(c == 1))
            stb, stk = STOB[si]
            self.cp("act", stb[0:rows, 0:256], PSB[pi][0:rows, 0:256], reads=[f"ps{pi}"], writes=[stk])
            self.dma(f"st_sto{si}", dst[l, :, :], stb[0:rows, 0:256], reads=[stk], is_out=True, eng=AUXQ)
        for si, (buf, idx, dst, lastrow, key) in enumerate(((self.SAT, 30, self.aso, 29, "SAT.new"),
                                                            (self.SCT, 2, self.cso, 1, "SCT.new"),
                                                            (self.SPT, 15, self.pso, 14, "SPT.new"))):
            pi = self.nps()
            for c in range(2):
                self.cp("dve", B["SM3"][:, c, :], buf[:, l, c, :, idx], reads=[key], writes=["SM3"])
                self.tr(PSB[pi][0:NS, c * 128:(c + 1) * 128], B["SM3"][:, c, :], self.CST[:, 0:128],
                        reads=["SM3", "CST"], writes=[f"ps{pi}"], sig=True)
            rs_ = si % 2
            self.cp("act", YA[0:NS, rs_, 0:256], PSB[pi][0:NS, 0:256], reads=[f"ps{pi}"], writes=[f"YA.{rs_}"])
            self.dma(f"st_row{rs_}", dst[l, :, lastrow, :], YA[0:NS, rs_, 0:256], reads=[f"YA.{rs_}"], is_out=True, eng=AUXQ)

    def ffn_layer(self, l):
        P = self.P
        PSB = self.PSB
        with ExitStack() as es:
            H2 = P.sb(f"H2_{l}", [128, 8, TT], BF16, es)
            ACTG = P.sb(f"ACTG_{l}", [128, 6, TT], BF16, es)
            self.WD = P.sb(f"WD_{l}", [128, 2, 6, D], BF16, es)
            SQ = P.sb(f"SQ2_{l}", [128, 2, 512], BF16, es)
            RS = P.sb(f"RS2_{l}", [128, 512], F32, es)
            SG = P.sb(f"SG_{l}", [128, 2, 512], F32, es)
            tiles = [(512 * t, 512, t) for t in range(4)] + [(T, NS, 4)]
            self.nps_n = 8
            pre_pair = (self.next_piece(), self.next_piece())
            for (c0, n, t) in tiles:
                self.rmsnorm_cols(84, l, (lambda k, c0=c0, n=n: H2[:, k, c0:c0 + n],), f"H2.{t}", c0, n, t, SQ, RS)
            sgi = 0
            for gi, grp in enumerate(FF_GROUPS):
                self.issue_copies(1)
                dk = []
                for li, p in enumerate(grp):
                    if pre_pair is not None:
                        (Wg, gk), (Wu, uk) = pre_pair
                        pre_pair = None
                    else:
                        Wg, gk = self.next_piece()
                        Wu, uk = self.next_piece()
                    for fc in range(2):
                        fi = 2 * li + fc
                        for (c0, n, t) in tiles:
                            hr = [f"H2.{t}.{k}" for k in range(8)]
                            pg = self.nps()
                            for k in range(8):
                                self.mm(PSB[pg][:, 0:n], Wg[:, k, fc * 128:(fc + 1) * 128], H2[:, k, c0:c0 + n],
                                        start=(k == 0), stop=(k == 7), reads=[gk] + hr, writes=[f"ps{pg}"], sig=(k == 7))
                            pu = self.nps()
                            for k in range(8):
                                self.mm(PSB[pu][:, 0:n], Wu[:, k, fc * 128:(fc + 1) * 128], H2[:, k, c0:c0 + n],
                                        start=(k == 0), stop=(k == 7), reads=[uk] + hr, writes=[f"ps{pu}"], sig=(k == 7))
                            s = sgi % 2
                            sgi += 1
                            self.act(SG[:, s, 0:n], PSB[pg][:, 0:n], AF.Silu, reads=[f"ps{pg}"], writes=[f"SG{s}"])
                            self.tt("dve", ACTG[:, fi, c0:c0 + n], PSB[pu][:, 0:n], SG[:, s, 0:n], ALU.mult,
                                    reads=[f"ps{pu}", f"SG{s}"], writes=[f"ACTG.{fi}.{t}"])
                    if li >= 1:
                        _, k_ = self.next_piece()
                        dk.append(k_)
                _, k_ = self.next_piece()
                dk.append(k_)
                nf = 2 * len(grp)
                g2 = gi % 2
                for oc in range(8):
                    for (c0, n, t) in tiles:
                        pi = self.nps()
                        for fi in range(nf):
                            self.mm(PSB[pi][:, 0:n], self.WD[:, g2, fi, oc * 128:(oc + 1) * 128], ACTG[:, fi, c0:c0 + n],
                                    start=(fi == 0), stop=(fi == nf - 1), reads=[dk[fi // 2], f"ACTG.{fi}.{t}"],
                                    writes=[f"ps{pi}"], sig=(fi == nf - 1))
                        self.tt("dve", self.XT[:, oc, c0:c0 + n], PSB[pi][:, 0:n], self.XT[:, oc, c0:c0 + n], ALU.add,
                                reads=[f"ps{pi}", f"XT.{oc}.{t}"], writes=[f"XT.{oc}.{t}"])
            P.drain_dmas()
            P.emit()
            self.WD = None

    def epilogue(self):
        P = self.P
        PSB = self.PSB
        with ExitStack() as es:
            self.nps_n = 8
            self.issue_copies(999)
            FG = P.sb("FG", [128, D], F32, es)
            YO = P.sb("YO", [128, 2, D], F32, es)
            JK = P.sb("JK", [128, 512], F32, es)
            SS = P.sb("SS", [128, 4], F32, es)
            self.dma("ld_fg", FG[:, :], self.fg[:, :], writes=["FG"])
            for a in range(17):
                rows = 128 if a < 16 else NS
                col = 128 * a
                t = a // 4 if a < 16 else 4
                s = a % 2
                pis = []
                for kk in range(2):
                    pi = self.nps()
                    pis.append(pi)
                    for j in range(4):
                        k = 4 * kk + j
                        self.tr(PSB[pi][0:rows, j * 128:(j + 1) * 128], self.XT[:, k, col:col + rows],
                                self.CST[:, 0:128], reads=[f"XT.{k}.{t}", "CST"], writes=[f"ps{pi}"], sig=(j == 3))
                    self.act(JK[0:rows, :], PSB[pi][0:rows, :], AF.Square, reads=[f"ps{pi}"], writes=["JK"],
                             accum_out=SS[0:rows, kk:kk + 1])
                self.tt("dve", SS[0:rows, 2:3], SS[0:rows, 0:1], SS[0:rows, 1:2], ALU.add, reads=["JK"], writes=["SS"])
                self.act(SS[0:rows, 2:3], SS[0:rows, 2:3], AF.Ln, reads=["SS"], writes=["SS"], scale=1.0 / D,
                         bias=self.EPSC[0:rows, 0:1])
                self.act(SS[0:rows, 3:4], SS[0:rows, 2:3], AF.Exp, reads=["SS"], writes=["SS"], scale=-0.5)
                for kk in range(2):
                    self.stt(YO[0:rows, s, 512 * kk:512 * kk + 512], PSB[pis[kk]][0:rows, :], SS[0:rows, 3:4],
                             FG[0:rows, 512 * kk:512 * kk + 512], ALU.mult, ALU.mult,
                             reads=[f"ps{pis[kk]}", "SS", "FG"], writes=[f"YO{s}"])
                dst = self.yp[col:col + 128, :] if a < 16 else self.ys[:, :]
                self.dma(f"st_y{s}", dst, YO[0:rows, s, :], reads=[f"YO{s}"], is_out=True)
            P.wait_all("sp", self.out_toks)
            P.wait_all("pool", self.out_toks)
            P.emit()

    def build(self):
        nc = self.nc
        self.declare()
        with self.es as es:
            P = self.P = Prog(nc, es)
            self.PSB = [P.ps(f"psb{i}", [128, 512]) for i in range(8)]
            self.XT = P.sb("XT", [128, 8, TT], F32)
            self.WSTG = P.sb("WSTG", [128, NSTG, 2048], F32)
            self.WBF = P.sb("WBF", [128, NRING, 2048], BF16)
            self.CST = P.sb("CST", [128, NCST], F32)
            self.PRM = P.sb("PRM", [128, 2 * NPRM], F32)
            self.POOLW = P.sb("POOLW", [128, 512], BF16)
            self.IDB = P.sb("IDB", [128, 128], BF16)
            self.MSK = P.sb("MSK", [128, 384], BF16)
            self.ONESB = P.sb("ONESB", [128, 128], BF16)
            self.EPSC = P.sb("EPSC", [128, 1], F32)
            self.SAT = P.sb("SAT", [128, 2, 2, NS, 31], F32)
            self.SCT = P.sb("SCT", [128, 2, 2, NS, 3], F32)
            self.SPT = P.sb("SPT", [128, 2, 2, NS, 16], F32)
            self.ONESF = self.CST[:, 128:256]
            self.ONESF_T = P.sb("ONESF", [128, 128], F32)
            self.ONESF = self.ONESF_T
            self.build_pieces()
            self.memset("pool", self.ONESF_T[:, :], 1.0, writes=["ONESF"])
            self.prologue()
            st = 0
            for l in range(2):
                for fn in (self.mixer_layer, self.ffn_layer):
                    st += 1
                    if st <= self.stages:
                        fn(l)
            self.epilogue()
        return nc


def _consts():
    c = np.zeros((128, NCST), np.float32)
    p = np.arange(128)
    c[:, 0:128] = np.eye(128, dtype=np.float32)
    c[:, 128:256] = (p[:, None] // 64 == p[None, :] // 64).astype(np.float32)
    c[:, 256:384] = 1.0 / 256.0
    q = np.arange(128)
    c[:, 384:512] = np.where(p[:, None] >= q[None, :], 0.0, NEGM)
    c[:, 512:640] = np.where(p[:, None] <= q[None, :], 0.0, NEGM)
    for b in range(4):
        i = np.arange(32)
        m = np.where((p[:, None] < 32 * b) | ((p[:, None] - 32 * b) <= i[None, :]), 0.0, NEGM)
        c[:, 640 + 32 * b:672 + 32 * b] = m
    t = np.arange(16)
    for ch in range(2):
        w = np.where(p < 64, (2, 8)[ch], (4, 16)[ch]).astype(np.float32)
        c[:, 768 + 16 * ch:784 + 16 * ch] = 1.0 / np.minimum(w[:, None], (t + 1)[None, :].astype(np.float32))
    return c


def _params(inp):
    prm = np.zeros((128, 2 * NPRM), np.float32)
    for l in range(2):
        o = l * NPRM
        for c in range(2):
            sl = slice(c * 128, c * 128 + 128)
            prm[:, o + 31 * c:o + 31 * c + 31] = inp["conv_a_w"][l][:, sl].T
            prm[:, o + 62 + c] = inp["conv_a_b"][l][sl]
            prm[:, o + 64 + c] = inp["ln_a_g"][l][sl]
            prm[:, o + 66 + c] = inp["ln_a_b"][l][sl]
            prm[:, o + 68 + 3 * c:o + 71 + 3 * c] = inp["conv_c_w"][l][:, sl].T
            prm[:, o + 74 + c] = inp["pool_scale"][l][sl]
        prm[:, o + 76:o + 84] = inp["norm1_g"][l].reshape(8, 128).T
        prm[:, o + 84:o + 92] = inp["norm2_g"][l].reshape(8, 128).T
    pw = np.zeros((128, 2, 2, 128), np.float32)
    for l in range(2):
        for c in range(2):
            for hh in range(2):
                g = 2 * c + hh
                pw[64 * hh:64 * hh + 64, l, c, 64 * hh:64 * hh + 64] = inp["pool_w"][l][g]
    return prm, pw.reshape(128, 512)


_NC_CACHE = {}


def _pack_weights(inp):
    wpc = np.empty((92, 128, 2048), np.float32)
    for l in range(2):
        o = 46 * l
        wi = inp["w_in"][l].reshape(8, 128, NPJ)
        wo = inp["w_out"][l].reshape(8, 128, D)
        wg = inp["w_gu"][l].reshape(8, 128, 2 * FF)
        for j in range(9):
            wpc[o + j] = wi[:, :, 256 * j:256 * j + 256].transpose(1, 0, 2).reshape(128, 2048)
        for j in range(4):
            wpc[o + 9 + j] = wo[:, :, 256 * j:256 * j + 256].transpose(1, 0, 2).reshape(128, 2048)
        for p in range(11):
            wpc[o + 13 + 2 * p] = wg[:, :, 256 * p:256 * p + 256].transpose(1, 0, 2).reshape(128, 2048)
            wpc[o + 14 + 2 * p] = wg[:, :, FF + 256 * p:FF + 256 * p + 256].transpose(1, 0, 2).reshape(128, 2048)
            wpc[o + 35 + p] = inp["w_down"][l][256 * p:256 * p + 256, :].reshape(2, 128, D).transpose(1, 0, 2).reshape(128, 2048)
    return wpc


def make_in_maps(inp):
    wpc = _pack_weights(inp)
    prm, pw = _params(inp)
    cst = _consts()
    fg = np.ascontiguousarray(np.broadcast_to(inp["final_g"].astype(np.float32)[None, :], (128, D)))
    f = lambda a: np.ascontiguousarray(a, dtype=np.float32)
    in_maps = []
    for i in range(8):
        sl = slice(NS * i, NS * i + NS)
        in_maps.append({
            "xp": f(inp["x_prompt"][i]),
            "xs": f(inp["x_sample"][sl, 0, :]),
            "ck": f(inp["cache_win_k"][:, sl].reshape(2, NS, 2048, 256)),
            "cv": f(inp["cache_win_v"][:, sl].reshape(2, NS, 2048, 256)),
            "sa": f(inp["state_conv_a"][:, sl]),
            "sc": f(inp["state_conv_c"][:, sl]),
            "spl": f(inp["state_pool"][:, sl]),
            "wpc": wpc,
            "prm": prm, "poolw": pw, "fg": fg, "cst": cst,
        })
    return in_maps


def kernel(**inp):
    inp = {k: np.asarray(v) for k, v in inp.items()}
    if "nc" not in _NC_CACHE:
        _NC_CACHE["nc"] = Kern().build()
    nc = _NC_CACHE["nc"]
    in_maps = make_in_maps(inp)
    res = run_bass_kernel_spmd(nc, in_maps, core_ids=list(range(8)))
    R = res.results
    cat = lambda k, ax: np.concatenate([r[k] for r in R], axis=ax)
    y_p = np.stack([r["yp"] for r in R], 0)
    y_s = cat("ys", 0).reshape(32, 1, D)
    k_p = np.stack([r["kp"] for r in R], 1).reshape(2, 8, T, 4, 64)
    v_p = np.stack([r["vp"] for r in R], 1).reshape(2, 8, T, 4, 64)
    a_p = np.stack([r["apo"] for r in R], 1)
    c_p = np.stack([r["cpo"] for r in R], 1)
    p_p = np.stack([r["ppo"] for r in R], 1)
    k_s = cat("ks", 1).reshape(2, 32, 2048, 4, 64)
    v_s = cat("vs", 1).reshape(2, 32, 2048, 4, 64)
    a_s = cat("aso", 1)
    c_s = cat("cso", 1)
    p_s = cat("pso", 1)
    return (y_p, y_s, k_p, v_p, a_p, c_p, p_p, k_s, v_s, a_s, c_s, p_s)
```

```python
from contextlib import ExitStack
import numpy as np
import concourse.bass as bass
import concourse.mybir as mybir
from concourse.bass_utils import run_bass_kernel_spmd

F32 = mybir.dt.float32
BF16 = mybir.dt.bfloat16
ALU = mybir.AluOpType
AF = mybir.ActivationFunctionType
AX = mybir.AxisListType

T = 2048
NS = 4
TT = T + NS
D = 1024
NPJ = 2304
FF = 2816
EPS = 1e-6
NEGM = -30000.0
NPRM = 92
NCST = 800
W_IN_ORDER = (4, 3, 2, 0, 1, 5, 7, 8, 6)
FF_GROUPS = ((0, 1, 2), (3, 4, 5), (6, 7, 8), (9, 10))
NSTG = 3
NRING = 3
CAST_ENG = "act"
AUXQ = "pool"


class Prog:
    ENGS = ("pe", "act", "dve", "pool", "sp")

    def __init__(self, nc, es):
        self.nc = nc
        self.es = es
        self.lists = {e: [] for e in self.ENGS}
        self.sem = {}
        self.cnt = {}
        self.waited = {e: {} for e in self.ENGS}
        self.lastw = {}
        self.reads = {}
        self.n_instr = 0
        for e in self.ENGS:
            self._mksem("c_" + e)

    def _mksem(self, name):
        if name not in self.sem:
            self.sem[name] = self.es.enter_context(self.nc.semaphore(name))
            self.cnt[name] = 0
        return self.sem[name]

    def sb(self, name, shape, dt, es=None):
        return (es or self.es).enter_context(self.nc.sbuf_tensor(name, list(shape), dt))

    def ps(self, name, shape, dt=F32):
        return self.es.enter_context(self.nc.psum_tensor(name, list(shape), dt))

    def _deps(self, eng, reads, writes, extra):
        need = {}

        def add(tok):
            if tok is None:
                return
            s, v = tok
            if eng == "pe" and s == "c_pe":
                return
            if need.get(s, 0) < v:
                need[s] = v

        for k in reads:
            add(self.lastw.get(k))
        for k in writes:
            add(self.lastw.get(k))
            for t in self.reads.get(k, ()):
                add(t)
        for t in extra:
            add(t)
        out = []
        w = self.waited[eng]
        for s, v in need.items():
            if w.get(s, 0) < v:
                w[s] = v
                out.append((s, v))
        return out

    def _commit(self, tok, reads, writes):
        for k in writes:
            self.lastw[k] = tok
            self.reads[k] = []
        for k in reads:
            lst = self.reads.setdefault(k, [])
            lst.append(tok)
            if len(lst) > 10:
                m = {}
                for s, v in lst:
                    if m.get(s, 0) < v:
                        m[s] = v
                self.reads[k] = list(m.items())

    def op(self, eng, fn, reads=(), writes=(), sig=True, extra=()):
        waits = self._deps(eng, reads, writes, extra)
        sname = "c_" + eng
        if sig:
            self.cnt[sname] += 1
            tok = (sname, self.cnt[sname])
        else:
            tok = (sname, self.cnt[sname] + 1)
        self.lists[eng].append((waits, fn, (sname, 1) if sig else None))
        self._commit(tok, reads, writes)
        self.n_instr += 1
        return tok

    def dma(self, semname, fn, reads=(), writes=(), eng="sp", extra=()):
        self._mksem(semname)
        waits = self._deps(eng, reads, writes, extra)
        self.cnt[semname] += 16
        tok = (semname, self.cnt[semname])
        self.lists[eng].append((waits, fn, (semname, 16)))
        self._commit(tok, reads, writes)
        self.n_instr += 1
        return tok

    def wait_all(self, eng, toks):
        need = {}
        w = self.waited[eng]
        for s, v in toks:
            if need.get(s, 0) < v and w.get(s, 0) < v:
                need[s] = v
        for s, v in need.items():
            w[s] = v
        if need:
            self.lists[eng].append((list(need.items()), None, None))

    def drain_dmas(self, exclude=("cpy", "wst")):
        toks = [(s, v) for s, v in self.cnt.items() if not s.startswith("c_") and v > 0
                and not any(s.startswith(x) for x in exclude)]
        for e in ("pe", "act", "dve", "pool"):
            self.wait_all(e, toks)

    def emit(self):
        nc = self.nc
        lists = self.lists
        self.lists = {e: [] for e in self.ENGS}
        sem = self.sem
        with nc.Block() as block:
            def mk(e):
                def body(handle):
                    for waits, fn, inc in lists[e]:
                        for s, v in waits:
                            handle.wait_ge(sem[s], v)
                        if fn is not None:
                            ins = fn(handle)
                            if inc is not None:
                                ins.then_inc(sem[inc[0]], inc[1])
                return body
            for e, dec in (("sp", block.sync), ("pe", block.tensor), ("act", block.scalar),
                           ("dve", block.vector), ("pool", block.gpsimd)):
                if lists[e]:
                    dec(mk(e))


class Kern:
    def __init__(self, stages=4):
        self.stages = stages
        self.nc = bass.Bass("TRN2", target_bir_lowering=False)
        self.es = ExitStack()
        self.P = None
        self.psrr = 0
        self.nps_n = 6
        self.out_toks = []

    def mm(self, out, lhsT, rhs, start=True, stop=True, reads=(), writes=(), sig=True, sgc=False):
        return self.P.op("pe", lambda h: h.matmul(out, lhsT=lhsT, rhs=rhs, start=start, stop=stop,
                                                  skip_group_check=sgc), reads, writes, sig)

    def tr(self, out, in_, ident, reads=(), writes=(), sig=True):
        return self.P.op("pe", lambda h: h.transpose(out=out, in_=in_, identity=ident), reads, writes, sig)

    def act(self, out, in_, func, reads=(), writes=(), scale=None, bias=None, accum_out=None):
        kw = {}
        if scale is not None:
            kw["scale"] = scale
        if bias is not None:
            kw["bias"] = bias
        if accum_out is not None:
            kw["accum_out"] = accum_out
        return self.P.op("act", lambda h: h.activation(out=out, in_=in_, func=func, **kw), reads, writes)

    def cp(self, eng, out, in_, reads=(), writes=()):
        if eng == "act":
            return self.P.op("act", lambda h: h.activation(out=out, in_=in_, func=AF.Copy), reads, writes)
        return self.P.op(eng, lambda h: h.tensor_copy(out=out, in_=in_), reads, writes)

    def tt(self, eng, out, in0, in1, op, reads=(), writes=()):
        return self.P.op(eng, lambda h: h.tensor_tensor(out=out, in0=in0, in1=in1, op=op), reads, writes)

    def ts(self, eng, out, in0, s1, op0, s2=None, op1=None, reads=(), writes=()):
        if op1 is None:
            return self.P.op(eng, lambda h: h.tensor_scalar(out=out, in0=in0, scalar1=s1, scalar2=None, op0=op0),
                             reads, writes)
        return self.P.op(eng, lambda h: h.tensor_scalar(out=out, in0=in0, scalar1=s1, scalar2=s2, op0=op0, op1=op1),
                         reads, writes)

    def stt(self, out, in0, scalar, in1, op0, op1, reads=(), writes=()):
        return self.P.op("dve", lambda h: h.scalar_tensor_tensor(out=out, in0=in0, scalar=scalar, in1=in1,
                                                                 op0=op0, op1=op1), reads, writes)

    def recip(self, out, in_, reads=(), writes=()):
        return self.P.op("dve", lambda h: h.reciprocal(out=out, in_=in_), reads, writes)

    def red(self, out, in_, reads=(), writes=()):
        return self.P.op("dve", lambda h: h.tensor_reduce(out=out, in_=in_, axis=AX.X, op=ALU.add), reads, writes)

    def memset(self, eng, ap, val, writes=()):
        return self.P.op(eng, lambda h: h.memset(ap, val), (), writes)

    def dma(self, sem, out, in_, reads=(), writes=(), eng="sp", is_out=False):
        tok = self.P.dma(sem, lambda h: h.dma_start(out=out, in_=in_), reads, writes, eng)
        if is_out:
            self.out_toks.append(tok)
        return tok

    def nps(self):
        i = self.psrr % self.nps_n
        self.psrr = (i + 1) % self.nps_n
        return i

    def declare(self):
        nc = self.nc
        di = lambda n, s: nc.dram_tensor(n, list(s), F32, kind="ExternalInput").ap()
        do = lambda n, s: nc.dram_tensor(n, list(s), F32, kind="ExternalOutput").ap()
        self.xp = di("xp", (T, D))
        self.xs = di("xs", (NS, D))
        self.ck = di("ck", (2, NS, 2048, 256))
        self.cv = di("cv", (2, NS, 2048, 256))
        self.sa = di("sa", (2, NS, 30, 256))
        self.sc = di("sc", (2, NS, 2, 256))
        self.spl = di("spl", (2, NS, 15, 256))
        self.wpc = di("wpc", (92, 128, 2048))
        self.prm = di("prm", (128, 2 * NPRM))
        self.poolw = di("poolw", (128, 2 * 2 * 128))
        self.fg = di("fg", (128, D))
        self.cst = di("cst", (128, NCST))
        self.yp = do("yp", (T, D))
        self.ys = do("ys", (NS, D))
        self.kp = do("kp", (2, T, 256))
        self.vp = do("vp", (2, T, 256))
        self.apo = do("apo", (2, 30, 256))
        self.cpo = do("cpo", (2, 2, 256))
        self.ppo = do("ppo", (2, 15, 256))
        self.ks = do("ks", (2, NS, 2048, 256))
        self.vs = do("vs", (2, NS, 2048, 256))
        self.aso = do("aso", (2, NS, 30, 256))
        self.cso = do("cso", (2, NS, 2, 256))
        self.pso = do("pso", (2, NS, 15, 256))

    def build_pieces(self):
        pcs = []
        for l in range(2):
            o = 46 * l
            for b in range(4):
                for j in W_IN_ORDER:
                    if j == 4 and b > 0:
                        continue
                    pcs.append(("in", self.wpc[o + j], None))
                if b < 3:
                    pcs.append(("in", self.wpc[o + 4], None))
                for j in range(4):
                    pcs.append(("out", self.wpc[o + 9 + j], None))
            for gi, grp in enumerate(FF_GROUPS):
                for li, p in enumerate(grp):
                    pcs.append(("g", self.wpc[o + 13 + 2 * p], None))
                    pcs.append(("u", self.wpc[o + 14 + 2 * p], None))
                    if li >= 1:
                        pcs.append(("d", self.wpc[o + 35 + grp[li - 1]], (gi % 2, li - 1)))
                pcs.append(("d", self.wpc[o + 35 + grp[-1]], (gi % 2, len(grp) - 1)))
        self.pieces = pcs
        self.pc_issued = 0
        self.pc_cast = 0
        self.pc_next = 0
        self.WD = None

    def _pc_issue(self, j):
        kind, src, _ = self.pieces[j]
        s = j % NSTG
        self.dma(f"wst{s}", self.WSTG[:, s, :], src, writes=[f"wst{s}"])

    def _pc_docast(self, j):
        kind, src, dd = self.pieces[j]
        s = j % NSTG
        if kind == "d":
            g2, li = dd
            out = self.WD[:, g2, 2 * li:2 * li + 2, :]
            in_ = self.WSTG[:, s, :].rearrange("p (c n) -> p c n", c=2)
            wk = [f"wd{g2}.{li}"]
        else:
            r = j % NRING
            out = self.WBF[:, r, :]
            in_ = self.WSTG[:, s, :]
            wk = [f"wbf{r}"]
        self.cp(CAST_ENG, out, in_, reads=[f"wst{s}"], writes=wk)

    def next_piece(self):
        i = self.pc_next
        self.pc_next += 1
        n = len(self.pieces)
        while self.pc_cast < min(n, i + 2):
            j = self.pc_cast
            while self.pc_issued < min(n, j + NSTG):
                self._pc_issue(self.pc_issued)
                self.pc_issued += 1
            if self.pieces[j][0] == "d" and self.WD is None:
                break
            self._pc_docast(j)
            self.pc_cast += 1
        assert self.pc_cast > i, "piece not cast"
        kind, _, dd = self.pieces[i]
        if kind == "d":
            return None, f"wd{dd[0]}.{dd[1]}"
        r = i % NRING
        return self.WBF[:, r, :].rearrange("p (k n) -> p k n", k=8), f"wbf{r}"

    def prm_ap(self, l, col):
        return self.PRM[:, l * NPRM + col:l * NPRM + col + 1]

    def rmsnorm_cols(self, gcol0, l, hbuf, hkey, c0, n, xkeyt, SQ, RS):
        hfn, = hbuf
        pi = self.nps()
        ps = self.PSB[pi]
        for k in range(8):
            s = k % 2
            self.act(SQ[:, s, 0:n], self.XT[:, k, c0:c0 + n], AF.Square, reads=[f"XT.{k}.{xkeyt}"],
                     writes=[f"SQ{s}"])
            self.mm(ps[:, 0:n], self.ONESB[:, :], SQ[:, s, 0:n], start=(k == 0), stop=(k == 7),
                    reads=[f"SQ{s}"], writes=[f"ps{pi}"], sig=True)
        self.act(RS[:, 0:n], ps[:, 0:n], AF.Ln, reads=[f"ps{pi}"], writes=["RS"], scale=1.0 / D, bias=self.EPSC[:, 0:1])
        self.act(RS[:, 0:n], RS[:, 0:n], AF.Exp, reads=["RS"], writes=["RS"], scale=-0.5)
        for k in range(8):
            self.stt(hfn(k), self.XT[:, k, c0:c0 + n], self.prm_ap(l, gcol0 + k), RS[:, 0:n], ALU.mult, ALU.mult,
                     reads=[f"XT.{k}.{xkeyt}", "RS"], writes=[f"{hkey}.{k}"])

    def prologue(self):
        P = self.P
        with ExitStack() as es:
            XS = P.sb("XS", [128, 4, D], F32, es)
            XSS = P.sb("XSS", [NS, D], F32, es)
            STA = P.sb("STA", [30, 2 * NS * 256], F32, es)
            STC = P.sb("STC", [2, 2 * NS * 256], F32, es)
            STP = P.sb("STP", [15, 2 * NS * 256], F32, es)
            PWS = P.sb("PWS", [128, 512], F32, es)
            self.dma("ld_cst", self.CST[:, :], self.cst[:, :], writes=["CST"])
            self.dma("ld_prm", self.PRM[:, :], self.prm[:, :], writes=["PRM"])
            self.dma("ld_pw", PWS[:, :], self.poolw[:, :], writes=["PWS"])
            self.cp("dve", self.POOLW[:, :], PWS[:, :], reads=["PWS"], writes=["POOLW"])
            self.cp("dve", self.IDB[:, :], self.CST[:, 0:128], reads=["CST"], writes=["IDB"])
            self.cp("dve", self.MSK[:, :], self.CST[:, 384:768], reads=["CST"], writes=["MSK"])
            self.memset("dve", self.ONESB[:, :], 1.0, writes=["ONESB"])
            self.memset("dve", self.EPSC[:, :], EPS, writes=["EPSC"])
            ctoks = [P.lastw[k] for k in ("CST", "PRM", "POOLW", "IDB", "MSK", "ONESB", "EPSC", "ONESF")]
            for e in ("pe", "act", "dve", "pool"):
                P.wait_all(e, ctoks)
            self._prologue_rest(XS, XSS, STA, STC, STP)
            P.drain_dmas()
            P.emit()

    def _cache_copies(self):
        big, small = [], []
        for l in range(2):
            for n in range(NS):
                for src, dst in ((self.ck, self.ks), (self.cv, self.vs)):
                    big.append((dst[l, n, 0:2047, :].rearrange("r c -> (r c)"),
                                src[l, n, 1:2048, :].rearrange("r c -> (r c)")))
                small.append((self.aso[l, n, 0:29, :].rearrange("r c -> (r c)"),
                              self.sa[l, n, 1:30, :].rearrange("r c -> (r c)")))
                small.append((self.cso[l, n, 0:1, :].rearrange("r c -> (r c)"),
                              self.sc[l, n, 1:2, :].rearrange("r c -> (r c)")))
                small.append((self.pso[l, n, 0:14, :].rearrange("r c -> (r c)"),
                              self.spl[l, n, 1:15, :].rearrange("r c -> (r c)")))
        self.copy_q = small + big
        self.issue_copies(len(small))

    def issue_copies(self, n):
        for _ in range(min(n, len(self.copy_q))):
            dst, src = self.copy_q.pop(0)
            self.dma("cpy", dst, src, eng="pool", is_out=True)

    def _prologue_rest(self, XS, XSS, STA, STC, STP):
        P = self.P
        if True:
            for (stg, src, rows, dstT, nm) in ((STA, self.sa, 30, self.SAT, "A"), (STC, self.sc, 2, self.SCT, "C"),
                                               (STP, self.spl, 15, self.SPT, "Pl")):
                self.dma("ld_st" + nm, stg[:, :].rearrange("r (g c) -> r g c", c=256),
                         src.rearrange("l n r c -> r (l n) c"), writes=["ST" + nm])
                pi = self.nps()
                ps = self.PSB[pi]
                i = 0
                for l in range(2):
                    for c in range(2):
                        for n in range(NS):
                            col = (l * NS + n) * 256 + c * 128
                            self.tr(ps[:, i * rows:(i + 1) * rows], stg[0:rows, col:col + 128],
                                    self.CST[0:rows, 0:rows], reads=["ST" + nm, "CST"], writes=[f"ps{pi}"],
                                    sig=(i == 15))
                            i += 1
                w = dstT.shape[-1]
                self.cp("dve", dstT[:, :, :, :, 0:rows].rearrange("p l c n r -> p (l c n) r"),
                        ps[:, 0:16 * rows].rearrange("p (g r) -> p g r", r=rows), reads=[f"ps{pi}"],
                        writes=["S" + nm + "T"])
            self.dma("ld_xs", XSS[:, :], self.xs[:, :], writes=["XSS"])
            pi = self.nps()
            ps = self.PSB[pi]
            for k in range(8):
                self.tr(ps[:, k * NS:(k + 1) * NS], XSS[0:NS, k * 128:(k + 1) * 128], self.CST[0:NS, 0:NS],
                        reads=["XSS", "CST"], writes=[f"ps{pi}"], sig=(k == 7))
            self.cp("dve", self.XT[:, :, T:TT], ps[:, 0:8 * NS].rearrange("p (k n) -> p k n", n=NS),
                    reads=[f"ps{pi}"], writes=[f"XT.{k}.4" for k in range(8)])
            for a in range(16):
                s = a % 4
                self.dma(f"ld_x{s}", XS[:, s, :], self.xp[128 * a:128 * a + 128, :], writes=[f"XS{s}"])
                for kk in range(2):
                    pi = self.nps()
                    ps = self.PSB[pi]
                    for j in range(4):
                        k = 4 * kk + j
                        self.tr(ps[:, j * 128:(j + 1) * 128], XS[:, s, k * 128:(k + 1) * 128], self.CST[:, 0:128],
                                reads=[f"XS{s}", "CST"], writes=[f"ps{pi}"], sig=(j == 3))
                    self.cp("act" if kk else "dve", self.XT[:, 4 * kk:4 * kk + 4, 128 * a:128 * a + 128],
                            ps[:, :].rearrange("p (j t) -> p j t", j=4), reads=[f"ps{pi}"],
                            writes=[f"XT.{k}.{a // 4}" for k in range(4 * kk, 4 * kk + 4)])

    def mixer_layer(self, l):
        P = self.P
        with ExitStack() as es:
            sb = lambda n, s, d: P.sb(f"{n}_{l}", s, d, es)
            Hq = sb("Hq", [128, 8, 516], BF16)
            MIXq = sb("MIXq", [128, 8, 516], BF16)
            QTq = sb("QTq", [128, 4, 512], BF16)
            KT = sb("KT", [128, 2, T], BF16)
            V1 = sb("V1", [128, 8, 256], BF16)
            V2 = sb("V2", [128, 8, 256], BF16)
            V3 = sb("V3", [128, 16, 256], BF16)
            GAq = sb("GAq", [128, 2, 542], F32)
            CPq = sb("CPq", [128, 2, 514], F32)
            PPq = sb("PPq", [128, 2, 527], F32)
            SQ = sb("SQ", [128, 2, 512], BF16)
            RS = sb("RS", [128, 512], F32)
            YA = sb("YA", [128, 2, 512], F32)
            MU = sb("MU", [128, 512], F32)
            RSTD = sb("RSTD", [128, 512], F32)
            T1 = sb("T1", [128, 512], F32)
            PT = sb("PT", [128, 4, 256], BF16)
            KVST = sb("KVST", [128, 4, 256], F32)
            CC = sb("CC", [128, 2, 512], F32)
            SA_ = sb("SA_", [128, 527], F32)
            SB_ = sb("SB_", [128, 527], F32)
            YDP = sb("YDP", [128, 2, 512], BF16)
            QTS = sb("QTS", [128, 2, NS], F32)
            KTS = sb("KTS", [128, 2, NS], F32)
            VTS = sb("VTS", [128, 2, NS], F32)
            CXS = sb("CXS", [128, 2, NS], F32)
            CBS = sb("CBS", [128, 2, NS], F32)
            YAS = sb("YAS", [128, 2, NS], F32)
            SQS = sb("SQS", [128, 2, NS], F32)
            SM1 = sb("SM1", [128, 2, NS, 31], F32)
            SM2 = sb("SM2", [128, 2, NS], F32)
            SM3 = sb("SM3", [128, 2, NS], F32)
            SM4 = sb("SM4", [128, 2, NS], F32)
            SM5 = sb("SM5", [128, 2, NS], F32)
            YDS = sb("YDS", [128, 2, NS], BF16)
            QREP = sb("QREP", [128, 256], F32)
            QBC = sb("QBC", [128, 2, 128], F32)
            KSB = sb("KSB", [128, 3, 256], F32)
            VSB = sb("VSB", [128, 3, 256], F32)
            PRD = sb("PRD", [128, 256], F32)
            SCS = sb("SCS", [128, 3, 4], F32)
            PEX = sb("PEX", [128, 3, 4], F32)
            UZ = sb("UZ", [128, 3, 4], F32)
            self.nps_n = 8
            if l == 0:
                self._cache_copies()
            self._mixer_body(l, locals())
            P.drain_dmas()
            P.emit()

    def _mixer_body(self, l, B):
        P = self.P
        Hq, MIXq, QTq, KT, V1, V2, V3 = B["Hq"], B["MIXq"], B["QTq"], B["KT"], B["V1"], B["V2"], B["V3"]
        GAq, CPq, PPq, SQ, RS, YA, MU, RSTD, T1 = (B[k] for k in ("GAq", "CPq", "PPq", "SQ", "RS", "YA",
                                                                "MU", "RSTD", "T1"))
        PT, KVST, CC, SA_, SB_, YDP = (B[k] for k in ("PT", "KVST", "CC", "SA_", "SB_", "YDP"))
        SIG = T1
        QTS, KTS, VTS, CXS, CBS, YAS, SQS = (B[k] for k in ("QTS", "KTS", "VTS", "CXS", "CBS", "YAS", "SQS"))
        PSB = self.PSB
        self.memset("dve", GAq[:, :, 0:30], 0.0, writes=["GAq.pre"])
        self.memset("dve", CPq[:, :, 0:2], 0.0, writes=["CPq.pre"])
        self.memset("dve", PPq[:, :, 0:15], 0.0, writes=["PPq.pre"])
        self.memset("pool", QTq[:, :, :], 0.0, writes=[f"QTq.{h_}" for h_ in range(4)])
        self.memset("pool", KT[:, :, :], 0.0, writes=["KT.i"])
        self.memset("pool", V3[:, :, :], 0.0, writes=["V3.i"])
        kvst_i = 0
        import os
        dbg = os.environ.get("KDBG", "9.9").split(".")
        dbg_b, dbg_l = int(dbg[0]), int(dbg[1])
        dbg_np = int(dbg[2]) if len(dbg) > 2 else 99
        def kv_tm(b, j, W, wkey):
            nonlocal kvst_i
            smp = (b == 3)
            c0 = 512 * b
            dst = self.kp if j == 3 else self.vp
            for a in range(4):
                pi = self.nps()
                for k in range(8):
                    self.mm(PSB[pi][:, 0:256], Hq[:, k, 128 * a:128 * a + 128], W[:, k, :], start=(k == 0),
                            stop=(k == 7), reads=[wkey] + hreads, writes=[f"ps{pi}"], sig=(k == 7))
                s = kvst_i % 4
                kvst_i += 1
                SK = os.environ.get("KSKIP", "")
                self.cp("act", KVST[:, s, :], PSB[pi][:, 0:256], reads=[f"ps{pi}"], writes=[f"KVST{s}"])
                if "vpdma" not in SK:
                  tok = self.dma(f"st_kv{s}", dst[l, c0 + 128 * a:c0 + 128 * a + 128, :], KVST[:, s, :],
                               reads=[f"KVST{s}"], writes=([f"vp.{b}.{a}"] if j == 4 else []), is_out=True, eng=AUXQ)
                if j == 4 and "v1" not in SK:
                    t1 = (4 * b + a) % 8
                    self.cp("dve", V1[:, t1, :], KVST[:, s, :], reads=[f"KVST{s}"], writes=[f"V1.{t1}"])
            if smp:
                pi = self.nps()
                for k in range(8):
                    self.mm(PSB[pi][0:NS, 0:256], Hq[:, k, 512:516], W[:, k, :], start=(k == 0),
                            stop=(k == 7), reads=[wkey] + hsreads, writes=[f"ps{pi}"], sig=(k == 7))
                ri = 0 if j == 3 else 1
                rb_, rk_ = (B["QREP"], "QREP") if j == 3 else (B["PRD"], "PRD")
                self.cp("act", rb_[0:NS, :], PSB[pi][0:NS, 0:256], reads=[f"ps{pi}"], writes=[rk_])
                self.dma(f"st_row{ri}", (self.ks if j == 3 else self.vs)[l, :, 2047, :], rb_[0:NS, :],
                         reads=[rk_], is_out=True, eng=AUXQ)
            if j == 4 and not os.environ.get("KSKIP_RB"):
                vsrc = self.vp[l, c0:c0 + 512, :]
                vrd = [f"vp.{b}.{a_}" for a_ in range(4)]
                s0 = (4 * b) % 8
                self.dma("ld_v2", V2[:, s0:s0 + 4, :], vsrc.rearrange("(p r) e -> p r e", r=4), reads=vrd,
                         writes=[f"V2.{s0 + r}" for r in range(4)], eng="pool")
                self.dma("ld_v3", V3[32 * b:32 * b + 32, :, :], vsrc.rearrange("(i r) e -> i r e", r=16),
                         reads=vrd + ["V3.i"], writes=[f"V3.{b}"], eng="pool")
                if smp:
                    for c in range(2):
                        pi = self.nps()
                        for k in range(8):
                            self.mm(PSB[pi][:, 0:NS], W[:, k, c * 128:(c + 1) * 128], Hq[:, k, 512:516],
                                    start=(k == 0), stop=(k == 7), reads=[wkey] + hsreads,
                                    writes=[f"ps{pi}"], sig=(k == 7))
                        self.cp("dve", VTS[:, c, :], PSB[pi][:, 0:NS], reads=[f"ps{pi}"], writes=["VTS"])


        for b in range(4):
            if b > dbg_b:
                break
            lvl = dbg_l if b == dbg_b else 9
            self.issue_copies(1)
            smp = (b == 3)
            ncol = 516 if smp else 512
            c0 = 512 * b
            def emit_norm1(bb):
                self.rmsnorm_cols(76, l, (lambda k: Hq[:, k, 0:512],), "Hq", 512 * bb, 512, bb, SQ, RS)
                if bb == 3:
                    self.rmsnorm_cols(76, l, (lambda k: Hq[:, k, 512:516],), "HqS", T, NS, 4, SQ, RS)
            if b == 0:
                emit_norm1(b)
            hreads = [f"Hq.{k}" for k in range(8)]
            hsreads = [f"HqS.{k}" for k in range(8)]
            if lvl < 2:
                break

            def proj_fm(W, wkey, c, evac, evac_s=None):
                pi = self.nps()
                for k in range(8):
                    self.mm(PSB[pi][:, 0:512], W[:, k, c * 128:(c + 1) * 128], Hq[:, k, 0:512], start=(k == 0),
                            stop=(k == 7), reads=[wkey] + hreads, writes=[f"ps{pi}"], sig=(k == 7))
                evac(PSB[pi][:, 0:512], f"ps{pi}")
                if smp and evac_s is not None:
                    pi = self.nps()
                    for k in range(8):
                        self.mm(PSB[pi][:, 0:NS], W[:, k, c * 128:(c + 1) * 128], Hq[:, k, 512:516], start=(k == 0),
                                stop=(k == 7), reads=[wkey] + hsreads, writes=[f"ps{pi}"], sig=(k == 7))
                    evac_s(PSB[pi][:, 0:NS], f"ps{pi}")

            for jn, j in enumerate(W_IN_ORDER):
                if j == 4 and b > 0:
                    continue
                W, wkey = self.next_piece()
                if j in (3, 4):
                    kv_tm(b, j, W, wkey)
                    if j == 3:
                        for c in range(2):
                            proj_fm(W, wkey, c,
                                    (lambda ps, pk, c=c: self.cp("act", KT[:, c, c0:c0 + 512], ps, reads=[pk, "KT.i"],
                                                                 writes=[f"KT.{b}"])),
                                    (lambda ps, pk, c=c: self.cp("dve", KTS[:, c, :], ps, reads=[pk], writes=["KTS"])))
                    continue
                for c in range(2):
                    if j == 2:
                        proj_fm(W, wkey, c,
                                (lambda ps, pk, c=c: (self.cp("act", QTq[0:64, 2 * c, :], ps[0:64, :], reads=[pk],
                                                              writes=[f"QTq.{2 * c}"]),
                                                      self.cp("act", QTq[64:128, 2 * c + 1, :], ps[64:128, :], reads=[pk],
                                                              writes=[f"QTq.{2 * c + 1}"]))),
                                (lambda ps, pk, c=c: self.cp("dve", QTS[:, c, :], ps, reads=[pk], writes=["QTS"])))
                    elif j == 0:
                        proj_fm(W, wkey, c,
                                (lambda ps, pk, c=c: self.cp("act", GAq[:, c, 30:542], ps, reads=[pk],
                                                             writes=[f"GAq.{c}"])),
                                (lambda ps, pk, c=c: self.cp("dve", self.SAT[:, l, c, :, 30], ps, reads=[pk],
                                                             writes=["SAT.new"])))
                    elif j == 1:
                        def ev(ps, pk, c=c):
                            self.act(SIG[:, :], ps, AF.Sigmoid, reads=[pk], writes=["T1"])
                            self.tt("dve", GAq[:, c, 30:542], GAq[:, c, 30:542], SIG[:, :], ALU.mult,
                                    reads=["T1", f"GAq.{c}"], writes=[f"GAq.{c}"])

                        def evs(ps, pk, c=c):
                            self.act(CBS[:, c, :], ps, AF.Sigmoid, reads=[pk], writes=["CBS"])
                            self.tt("dve", self.SAT[:, l, c, :, 30], self.SAT[:, l, c, :, 30], CBS[:, c, :], ALU.mult,
                                    reads=["CBS", "SAT.new"], writes=["SAT.new"])
                        proj_fm(W, wkey, c, ev, evs)
                    elif j == 5:
                        proj_fm(W, wkey, c,
                                (lambda ps, pk, c=c: self.cp("act", CPq[:, c, 2:514], ps, reads=[pk],
                                                             writes=[f"CPq.{c}"])),
                                (lambda ps, pk, c=c: self.cp("dve", CXS[:, c, :], ps, reads=[pk], writes=["CXS"])))
                    elif j == 7:
                        def ev(ps, pk, c=c):
                            self.tt("dve", CPq[:, c, 2:514], ps, CPq[:, c, 2:514], ALU.mult, reads=[pk, f"CPq.{c}"],
                                    writes=[f"CPq.{c}"])
                            self.ts("dve", CC[:, c, :], CPq[:, c, 0:512], self.prm_ap(l, 68 + 3 * c), ALU.mult,
                                    reads=[f"CPq.{c}", "CPq.pre"], writes=[f"CC.{c}"])
                            for k in (1, 2):
                                self.stt(CC[:, c, :], CPq[:, c, k:k + 512], self.prm_ap(l, 68 + 3 * c + k), CC[:, c, :],
                                         ALU.mult, ALU.add, reads=[f"CPq.{c}", "CPq.pre", f"CC.{c}"],
                                         writes=[f"CC.{c}"])

                        def evs(ps, pk, c=c):
                            self.tt("dve", self.SCT[:, l, c, :, 2], ps, CXS[:, c, :], ALU.mult, reads=[pk, "CXS"],
                                    writes=["SCT.new"])
                        proj_fm(W, wkey, c, ev, evs)
                    elif j == 6:
                        def evs(ps, pk, c=c):
                            self.tt("dve", B["SM1"][:, c, :, 0:3], self.SCT[:, l, c, :, :],
                                    self.PRM[:, l * NPRM + 68 + 3 * c:l * NPRM + 71 + 3 * c].unsqueeze(1).to_broadcast(
                                        [128, NS, 3]), ALU.mult, reads=["SCT.new", "SCT"], writes=["SM1"])
                            self.red(B["SM2"][:, c, :], B["SM1"][:, c, :, 0:3], reads=["SM1"], writes=["SM2"])
                            self.tt("dve", MIXq[:, 4 + c, 512:516], ps, B["SM2"][:, c, :], ALU.mult,
                                    reads=[pk, "SM2"], writes=[f"MIXS.{4 + c}"])
                        proj_fm(W, wkey, c,
                                (lambda ps, pk, c=c: self.tt("dve", MIXq[:, 4 + c, 0:512], ps, CC[:, c, :], ALU.mult,
                                                             reads=[pk, f"CC.{c}"], writes=[f"MIXq.{4 + c}"])),
                                evs)
                    elif j == 8:
                        proj_fm(W, wkey, c,
                                (lambda ps, pk, c=c: self.cp("act", PPq[:, c, 15:527], ps, reads=[pk],
                                                             writes=[f"PPq.{c}"])),
                                (lambda ps, pk, c=c: self.cp("dve", self.SPT[:, l, c, :, 15], ps, reads=[pk],
                                                             writes=["SPT.new"])))
                if j == 8:
                    self._mixer_D(l, b, B, part=0)
            if lvl < 3:
                break
            if b < 3:
                emit_norm1(b + 1)
            conv_ops = self._convA_ops(l, B)
            self._mixer_D(l, b, B, part=1)
            self.nps_n = 4
            ptc = [0]
            nq_ = (len(conv_ops) + 3) // 4
            if smp:
                self._sample_mix(l, B, part="pre")
            for h in range(4):
                self._attention_head(l, b, B, h, ptc, conv_ops[h * nq_:(h + 1) * nq_])
                if smp:
                    self._sample_mix(l, B, part=h)
            self.nps_n = 8
            self._mixer_A_ln(l, B)
            if b < 3:
                Wv, wvk = self.next_piece()
                kv_tm(b + 1, 4, Wv, wvk)
            if lvl < 4:
                break
            if smp:
                self._state_outputs(l, B)
            if b < 3:
                self.cp("dve", GAq[:, :, 0:30], GAq[:, :, 512:542], reads=["GAq.0", "GAq.1", "GAq.pre"],
                        writes=["GAq.pre"])
                self.cp("dve", CPq[:, :, 0:2], CPq[:, :, 512:514], reads=["CPq.0", "CPq.1", "CPq.pre"],
                        writes=["CPq.pre"])
                self.cp("dve", PPq[:, :, 0:15], PPq[:, :, 512:527], reads=["PPq.0", "PPq.1", "PPq.pre"],
                        writes=["PPq.pre"])
            mreads = [f"MIXq.{k}" for k in range(8)]
            msreads = [f"MIXS.{k}" for k in range(8)]
            for j in range(4):
                W, wkey = self.next_piece()
                for c in range(2):
                    oc = 2 * j + c
                    pi = self.nps()
                    for k in range(8):
                        self.mm(PSB[pi][:, 0:512], W[:, k, c * 128:(c + 1) * 128], MIXq[:, k, 0:512], start=(k == 0),
                                stop=(k == 7), reads=[wkey] + mreads, writes=[f"ps{pi}"], sig=(k == 7))
                    self.tt("dve", self.XT[:, oc, c0:c0 + 512], PSB[pi][:, 0:512], self.XT[:, oc, c0:c0 + 512], ALU.add,
                            reads=[f"ps{pi}", f"XT.{oc}.{b}"], writes=[f"XT.{oc}.{b}"])
                    if smp:
                        pi = self.nps()
                        for k in range(8):
                            self.mm(PSB[pi][:, 0:NS], W[:, k, c * 128:(c + 1) * 128], MIXq[:, k, 512:516],
                                    start=(k == 0), stop=(k == 7), reads=[wkey] + msreads, writes=[f"ps{pi}"],
                                    sig=(k == 7))
                        self.tt("dve", self.XT[:, oc, T:TT], PSB[pi][:, 0:NS], self.XT[:, oc, T:TT], ALU.add,
                                reads=[f"ps{pi}", f"XT.{oc}.4"], writes=[f"XT.{oc}.4"])

    def _ln_silu(self, l, n, ya_fn, sq_fn, out_fn, yakeys, outkeys, MU, RSTD, T1, sqkeys=("CC.0", "CC.1")):
        PSB = self.PSB
        p1 = self.nps()
        p2 = self.nps()
        for c in range(2):
            self.act(sq_fn(c), ya_fn(c), AF.Square, reads=[yakeys[c]], writes=[sqkeys[c]])
        for c in range(2):
            self.mm(PSB[p1][:, 0:n], self.CST[:, 256:384], ya_fn(c), start=(c == 0), stop=(c == 1),
                    reads=[yakeys[c], "CST"], writes=[f"ps{p1}"], sig=(c == 1))
        for c in range(2):
            self.mm(PSB[p2][:, 0:n], self.CST[:, 256:384], sq_fn(c), start=(c == 0), stop=(c == 1),
                    reads=[sqkeys[c], "CST"], writes=[f"ps{p2}"], sig=(c == 1))
        self.cp("act", MU[:, 0:n], PSB[p1][:, 0:n], reads=[f"ps{p1}"], writes=["MU"])
        self.tt("dve", RSTD[:, 0:n], MU[:, 0:n], MU[:, 0:n], ALU.mult, reads=["MU"], writes=["RSTD"])
        self.tt("dve", RSTD[:, 0:n], PSB[p2][:, 0:n], RSTD[:, 0:n], ALU.subtract, reads=[f"ps{p2}", "RSTD"],
                writes=["RSTD"])
        self.ts("dve", RSTD[:, 0:n], RSTD[:, 0:n], 0.0, ALU.max, reads=["RSTD"], writes=["RSTD"])
        self.act(RSTD[:, 0:n], RSTD[:, 0:n], AF.Ln, reads=["RSTD"], writes=["RSTD"], bias=self.EPSC[:, 0:1])
        self.act(RSTD[:, 0:n], RSTD[:, 0:n], AF.Exp, reads=["RSTD"], writes=["RSTD"], scale=-0.5)
        for c in range(2):
            self.tt("dve", T1[:, 0:n], ya_fn(c), MU[:, 0:n], ALU.subtract, reads=[yakeys[c], "MU"], writes=["T1"])
            self.tt("dve", T1[:, 0:n], T1[:, 0:n], RSTD[:, 0:n], ALU.mult, reads=["T1", "RSTD"], writes=["T1"])
            self.act(out_fn(c), T1[:, 0:n], AF.Silu, reads=["T1"], writes=[outkeys[c]],
                     scale=self.prm_ap(l, 64 + c), bias=self.prm_ap(l, 66 + c))

    def _convA_ops(self, l, B):
        GAq, YA = B["GAq"], B["YA"]
        ops = []
        for c in range(2):
            rd = [f"GAq.{c}", "GAq.pre"]
            ops.append(lambda c=c, rd=rd: self.ts("dve", YA[:, c, :], GAq[:, c, 0:512], self.prm_ap(l, 31 * c), ALU.mult,
                                                  s2=self.prm_ap(l, 62 + c), op1=ALU.add, reads=rd, writes=[f"YA.{c}"]))
            for k in range(1, 31):
                ops.append(lambda c=c, rd=rd, k=k: self.stt(YA[:, c, :], GAq[:, c, k:k + 512], self.prm_ap(l, 31 * c + k),
                                                            YA[:, c, :], ALU.mult, ALU.add, reads=rd + [f"YA.{c}"],
                                                            writes=[f"YA.{c}"]))
        return ops

    def _mixer_A_ln(self, l, B):
        YA, SQF, MIXq = B["YA"], B["CC"], B["MIXq"]
        self._ln_silu(l, 512, lambda c: YA[:, c, :], lambda c: SQF[:, c, :], lambda c: MIXq[:, c, 0:512],
                      ["YA.0", "YA.1"], ["MIXq.0", "MIXq.1"], B["MU"], B["RSTD"], B["T1"])

    def _mixer_D(self, l, b, B, part):
        PPq, SA_, SB_, YDP, MIXq = B["PPq"], B["SA_"], B["SB_"], B["YDP"], B["MIXq"]
        PSB = self.PSB
        if part == 1:
            for c in range(2):
                pi = self.nps()
                self.mm(PSB[pi][:, 0:512], self.POOLW[:, (2 * l + c) * 128:(2 * l + c + 1) * 128], YDP[:, c, :],
                        reads=[f"YDP.{c}", "POOLW"], writes=[f"ps{pi}"])
                self.ts("dve", MIXq[:, 6 + c, 0:512], PSB[pi][:, 0:512], self.prm_ap(l, 74 + c), ALU.mult,
                        reads=[f"ps{pi}"], writes=[f"MIXq.{6 + c}"])
            return
        for c in range(2):
            rd = [f"PPq.{c}", "PPq.pre"]
            pp = PPq[:, c, :]
            self.tt("dve", SA_[:, 1:527], pp[:, 1:527], pp[:, 0:526], ALU.add, reads=rd, writes=["SA_"])
            self.tt("dve", SB_[:, 3:527], SA_[:, 3:527], SA_[:, 1:525], ALU.add, reads=["SA_"], writes=["SB_"])
            if c == 0:
                lo, hi, wlo, whi = SA_, SB_, 2.0, 4.0
                keys = ["SA_", "SB_"]
            else:
                self.tt("dve", SA_[:, 7:527], SB_[:, 7:527], SB_[:, 3:523], ALU.add, reads=["SB_", "SA_"], writes=["SA_"])
                self.tt("dve", SB_[:, 15:527], SA_[:, 15:527], SA_[:, 7:519], ALU.add, reads=["SA_", "SB_"],
                        writes=["SB_"])
                lo, hi, wlo, whi = SA_, SB_, 8.0, 16.0
                keys = ["SA_", "SB_"]
            self.stt(YDP[0:64, c, :], lo[0:64, 15:527], 1.0 / wlo, pp[0:64, 15:527], ALU.mult, ALU.subtract,
                     reads=keys + rd, writes=[f"YDP.{c}"])
            self.stt(YDP[64:128, c, :], hi[64:128, 15:527], 1.0 / whi, pp[64:128, 15:527], ALU.mult, ALU.subtract,
                     reads=keys + rd, writes=[f"YDP.{c}"])
            if b == 0:
                for (buf, p0) in ((lo, 0), (hi, 64)):
                    self.tt("dve", B["T1"][p0:p0 + 64, 0:16], buf[p0:p0 + 64, 15:31],
                            self.CST[p0:p0 + 64, 768 + 16 * c:784 + 16 * c], ALU.mult, reads=keys + ["CST"],
                            writes=["T1"])
                    self.tt("dve", YDP[p0:p0 + 64, c, 0:16], B["T1"][p0:p0 + 64, 0:16], pp[p0:p0 + 64, 15:31],
                            ALU.subtract, reads=["T1"] + rd, writes=[f"YDP.{c}"])

    def _attention_head(self, l, b, B, h, ptc, mid_ops):
        QTq, KT, V1, V2, V3, PT, RCP, MIXq = (B[k] for k in ("QTq", "KT", "V1", "V2", "V3", "PT", "T1", "MIXq"))
        PSB = self.PSB
        MSK = self.MSK
        c0 = 512 * b
        bu, bz = (4, 5) if h % 2 == 0 else (6, 7)
        if True:
            c = h // 2
            pb = (h % 2) * 64
            jobs = []
            for qq in range(4):
                qb = 4 * b + qq
                kt = []
                if qb > 0:
                    kt.append((slice(128 * (qb - 1), 128 * qb), V1[:, (qb - 1) % 8, :], f"V1.{(qb - 1) % 8}",
                               f"KT.{(qb - 1) // 4}"))
                kt.append((slice(128 * qb, 128 * qb + 128), V1[:, qb % 8, :], f"V1.{qb % 8}", f"KT.{b}"))
                jobs.append((kt, 128, slice(128 * qq, 128 * qq + 128), 128,
                             MSK[:, 0:256] if qb > 0 else MSK[:, 128:256], None))
            for r4 in range(4):
                kt = []
                if b > 0:
                    s = (4 * (b - 1) + r4) % 8
                    kt.append((slice(512 * (b - 1) + r4, 512 * b, 4), V2[:, s, :], f"V2.{s}", f"KT.{b - 1}"))
                s = (4 * b + r4) % 8
                kt.append((slice(512 * b + r4, 512 * b + 512, 4), V2[:, s, :], f"V2.{s}", f"KT.{b}"))
                jobs.append((kt, 128, slice(r4, 512, 4), 128, MSK[:, 0:256] if b > 0 else MSK[:, 128:256], None))
            for g4 in range(4):
                kt = [(slice(r16, T, 16), V3[:, r16, :], None, None) for r16 in range(4 * g4, 4 * g4 + 4)]
                qsl = [slice(r16, 512, 16) for r16 in range(4 * g4, 4 * g4 + 4)]
                jobs.append((kt, 128, qsl, 32,
                             MSK[:, 256 + 32 * b:256 + 32 * b + 32].unsqueeze(1).to_broadcast([128, 4, 32]), g4))
            ktall = [f"KT.{i}" for i in range(b + 1)] + ["KT.i"]
            v3all = [f"V3.{i}" for i in range(b + 1)] + ["V3.i"]
            state = {"first": True}

            def qk(job):
                kt, kp, qs, nq, msk, g4 = job
                pi = self.nps()
                ncols = nq * len(kt)
                mout = PSB[pi][0:kp, 0:ncols]
                if g4 is not None:
                    mout = mout.rearrange("p (j i) -> p j i", j=4)
                self.mm(mout, self.IDB[0:kp, 0:kp], msk, start=True, stop=False,
                        reads=["IDB", "MSK"], writes=[f"ps{pi}"], sig=False)
                for i, (ks, vt, vk, kk) in enumerate(kt):
                    self.mm(PSB[pi][0:kp, i * nq:(i + 1) * nq], KT[:, c, ks], QTq[:, h, qs[i] if g4 is not None else qs],
                            start=False, stop=(i == len(kt) - 1), reads=[f"QTq.{h}"] + ([kk, "KT.i"] if kk else ktall),
                            writes=[f"ps{pi}"], sig=(i == len(kt) - 1))
                s = ptc[0] % 4
                ptc[0] += 1
                self.act(PT[0:kp, s, 0:ncols], PSB[pi][0:kp, 0:ncols], AF.Exp, reads=[f"ps{pi}"], writes=[f"PT{s}"],
                         scale=0.125)
                return s

            def pv(job, s):
                kt, kp, qs, nq, msk, g4 = job
                for i, (ks, vt, vk, kk) in enumerate(kt):
                    qo = qs[i] if g4 is not None else qs
                    self.mm(PSB[bu][0:64, qo], vt[0:kp, h * 64:(h + 1) * 64] if vk else vt[:, h * 64:(h + 1) * 64],
                            PT[0:kp, s, i * nq:(i + 1) * nq], start=state["first"], stop=True,
                            reads=[f"PT{s}"] + ([vk] if vk else v3all), writes=[f"ps{bu}"], sig=False, sgc=True)
                    self.mm(PSB[bz][0:64, qo], self.ONESB[0:kp, 0:64], PT[0:kp, s, i * nq:(i + 1) * nq],
                            start=state["first"], stop=True, reads=[f"PT{s}", "ONESB"], writes=[f"ps{bz}"],
                            sig=(i == len(kt) - 1), sgc=True)
                    state["first"] = False

            prev = None
            for job in jobs:
                s = qk(job)
                if prev is not None:
                    pv(*prev)
                prev = (job, s)
            pv(*prev)
            for f_ in mid_ops:
                f_()
            self.act(RCP[0:64, :], PSB[bz][0:64, :], AF.Ln, reads=[f"ps{bz}"], writes=["T1"])
            self.act(RCP[0:64, :], RCP[0:64, :], AF.Exp, reads=["T1"], writes=["T1"], scale=-1.0)
            self.tt("dve", MIXq[pb:pb + 64, 2 + c, 0:512], PSB[bu][0:64, :], RCP[0:64, :], ALU.mult,
                    reads=[f"ps{bu}", "T1"], writes=[f"MIXq.{2 + c}"])

    def _sample_mix(self, l, B, part):
        PSB = self.PSB
        MIXq, YAS, SQS, SM1, SM2, SM3, SM4, SM5, YDS = (B[k] for k in ("MIXq", "YAS", "SQS", "SM1", "SM2", "SM3", "SM4",
                                                                     "SM5", "YDS"))
        QTS, KTS, VTS, QREP, QBC, KSB, VSB, PRD, SCS, PEX, UZ = (B[k] for k in ("QTS", "KTS", "VTS", "QREP", "QBC", "KSB",
                                                                             "VSB", "PRD", "SCS", "PEX", "UZ"))
        if part == "pre":
            self._sample_pre(l, B)
        else:
            self._sample_seq(l, B, part)

    def _sample_pre(self, l, B):
        PSB = self.PSB
        MIXq, YAS, SQS, SM1, SM2, SM3, SM4, SM5, YDS = (B[k] for k in ("MIXq", "YAS", "SQS", "SM1", "SM2", "SM3", "SM4",
                                                                     "SM5", "YDS"))
        QTS, KTS, VTS, QREP, QBC, KSB, VSB, PRD, SCS, PEX, UZ = (B[k] for k in ("QTS", "KTS", "VTS", "QREP", "QBC", "KSB",
                                                                             "VSB", "PRD", "SCS", "PEX", "UZ"))
        for c in range(2):
            self.tt("dve", SM1[:, c, :, :], self.SAT[:, l, c, :, :],
                    self.PRM[:, l * NPRM + 31 * c:l * NPRM + 31 * c + 31].unsqueeze(1).to_broadcast([128, NS, 31]),
                    ALU.mult, reads=["SAT.new", "SAT", "SAT"], writes=["SM1"])
            self.red(YAS[:, c, :], SM1[:, c, :, :], reads=["SM1"], writes=["YAS"])
            self.ts("dve", YAS[:, c, :], YAS[:, c, :], self.prm_ap(l, 62 + c), ALU.add, reads=["YAS"], writes=["YAS"])
        self._ln_silu(l, NS, lambda c: YAS[:, c, :], lambda c: SQS[:, c, :], lambda c: MIXq[:, c, 512:516],
                      ["YAS", "YAS"], ["MIXS.0", "MIXS.1"], B["MU"], B["RSTD"], B["T1"], sqkeys=("SQS", "SQS"))
        for c in range(2):
            for (p0, w) in ((0, (2, 8)[c]), (64, (4, 16)[c])):
                self.red(SM2[p0:p0 + 64, c, :], self.SPT[p0:p0 + 64, l, c, :, 16 - w:16], reads=["SPT.new", "SPT"],
                         writes=["SM2"])
                self.stt(YDS[p0:p0 + 64, c, :], SM2[p0:p0 + 64, c, :], 1.0 / w, self.SPT[p0:p0 + 64, l, c, :, 15],
                         ALU.mult, ALU.subtract, reads=["SM2", "SPT.new"], writes=["YDS"])
            pi = self.nps()
            self.mm(PSB[pi][:, 0:NS], self.POOLW[:, (2 * l + c) * 128:(2 * l + c + 1) * 128], YDS[:, c, :],
                    reads=["YDS", "POOLW"], writes=[f"ps{pi}"])
            self.ts("dve", MIXq[:, 6 + c, 512:516], PSB[pi][:, 0:NS], self.prm_ap(l, 74 + c), ALU.mult,
                    reads=[f"ps{pi}"], writes=[f"MIXS.{6 + c}"])
        self.tt("dve", SM3[:, :, :], QTS[:, :, :], KTS[:, :, :], ALU.mult, reads=["QTS", "KTS"], writes=["SM3"])
        pi = self.nps()
        for c in range(2):
            self.mm(PSB[pi][:, c * NS:(c + 1) * NS], self.CST[:, 128:256], SM3[:, c, :], reads=["SM3", "CST"],
                    writes=[f"ps{pi}"], sig=(c == 1))
        self.act(SM4[:, :, :], PSB[pi][:, 0:2 * NS].rearrange("p (c n) -> p c n", n=NS), AF.Exp, reads=[f"ps{pi}"],
                 writes=["SM4"], scale=0.125)
        self.ts("dve", SM4[:, :, :], SM4[:, :, :], 3.0, ALU.mult, reads=["SM4"], writes=["SM4"])
        self.tt("dve", SM5[:, :, :], SM4[:, :, :], VTS[:, :, :], ALU.mult, reads=["SM4", "VTS"], writes=["SM5"])

    def _sample_seq(self, l, B, n):
        PSB = self.PSB
        MIXq, YAS, SQS, SM1, SM2, SM3, SM4, SM5, YDS = (B[k] for k in ("MIXq", "YAS", "SQS", "SM1", "SM2", "SM3", "SM4",
                                                                     "SM5", "YDS"))
        QTS, KTS, VTS, QREP, QBC, KSB, VSB, PRD, SCS, PEX, UZ = (B[k] for k in ("QTS", "KTS", "VTS", "QREP", "QBC", "KSB",
                                                                             "VSB", "PRD", "SCS", "PEX", "UZ"))
        if True:
            for c in range(2):
                self.cp("dve", QBC[:, c, :], QTS[:, c, n:n + 1].to_broadcast([128, 128]), reads=["QTS"], writes=["QBC"])
            pi = self.nps()
            for c in range(2):
                self.mm(PSB[pi][:, c * 128:(c + 1) * 128], QBC[:, c, :], self.CST[:, 0:128], reads=["QBC", "CST"],
                        writes=[f"ps{pi}"], sig=(c == 1))
            self.cp("act", QREP[:, :], PSB[pi][:, 0:256], reads=[f"ps{pi}"], writes=["QREP"])
            for ci, d in enumerate((1, 4, 16)):
                r0 = 2048 - 128 * d
                self.dma(f"ld_ks{ci}", KSB[:, ci, :], self.ck[l, n, r0:2048:d, :], writes=[f"KSB{ci}"], eng=AUXQ)
                self.dma(f"ld_vs{ci}", VSB[:, ci, :], self.cv[l, n, r0:2048:d, :], writes=[f"VSB{ci}"], eng=AUXQ)
                self.tt("dve", PRD[:, :], KSB[:, ci, :], QREP[:, :], ALU.mult, reads=[f"KSB{ci}", "QREP"],
                        writes=["PRD"])
                self.red(SCS[:, ci, :], PRD[:, :].rearrange("p (h e) -> p h e", e=64), reads=["PRD"],
                         writes=[f"SCS{ci}"])
                self.act(PEX[:, ci, :], SCS[:, ci, :], AF.Exp, reads=[f"SCS{ci}"], writes=[f"PEX{ci}"], scale=0.125)
            pi = self.nps()
            ps = PSB[pi]
            for g in range(3):
                for ci in range(3):
                    lhs = self.ONESF[:, :] if g == 2 else VSB[:, ci, g * 128:(g + 1) * 128]
                    self.mm(ps[:, 4 * g:4 * g + 4], lhs, PEX[:, ci, :], start=(g == 0 and ci == 0), stop=(ci == 2),
                            reads=[f"VSB{ci}", f"PEX{ci}", "ONESF"], writes=[f"ps{pi}"], sig=(g == 2 and ci == 2),
                            sgc=True)
            self.cp("act", UZ[:, :, :], ps[:, 0:12].rearrange("p (g h) -> p g h", h=4), reads=[f"ps{pi}"],
                    writes=["UZ"])
            for c in range(2):
                for hh in range(2):
                    h = 2 * c + hh
                    p0 = 64 * hh
                    self.tt("dve", SM2[p0:p0 + 64, c, 0:1], UZ[p0:p0 + 64, c, h:h + 1], SM5[p0:p0 + 64, c, n:n + 1],
                            ALU.add, reads=["UZ", "SM5"], writes=["SM2"])
                    self.tt("dve", SM2[p0:p0 + 64, c, 1:2], UZ[p0:p0 + 64, 2, h:h + 1], SM4[p0:p0 + 64, c, n:n + 1],
                            ALU.add, reads=["UZ", "SM4"], writes=["SM2"])
                    self.recip(SM2[p0:p0 + 64, c, 1:2], SM2[p0:p0 + 64, c, 1:2], reads=["SM2"], writes=["SM2"])
                    self.tt("dve", MIXq[p0:p0 + 64, 2 + c, 512 + n:513 + n], SM2[p0:p0 + 64, c, 0:1],
                            SM2[p0:p0 + 64, c, 1:2], ALU.mult, reads=["SM2"], writes=[f"MIXS.{2 + c}"])

    def _state_outputs(self, l, B):
        PSB = self.PSB
        YA = B["YA"]
        STOB = ((B["SA_"], "SA_"), (B["SB_"], "SB_"), (B["MU"], "MU"))
        GAq, CPq, PPq = B["GAq"], B["CPq"], B["PPq"]
        for si, (buf, off, rows, dst, key) in enumerate(((GAq, 30, 30, self.apo, "GAq"), (CPq, 2, 2, self.cpo, "CPq"),
                                                         (PPq, 15, 15, self.ppo, "PPq"))):
            pi = self.nps()
            for c in range(2):
                self.tr(PSB[pi][0:rows, c * 128:(c + 1) * 128], buf[:, c, off + 512 - rows:off + 512],
                        self.CST[:, 0:128], reads=[f"{key}.{c}", "CST"], writes=[f"ps{pi}"], sig=(c == 1))
            stb, stk = STOB[si]
            self.cp("act", stb[0:rows, 0:256], PSB[pi][0:rows, 0:256], reads=[f"ps{pi}"], writes=[stk])
            self.dma(f"st_sto{si}", dst[l, :, :], stb[0:rows, 0:256], reads=[stk], is_out=True, eng=AUXQ)
        for si, (buf, idx, dst, lastrow, key) in enumerate(((self.SAT, 30, self.aso, 29, "SAT.new"),
                                                            (self.SCT, 2, self.cso, 1, "SCT.new"),
                                                            (self.SPT, 15, self.pso, 14, "SPT.new"))):
            pi = self.nps()
            for c in range(2):
                self.cp("dve", B["SM3"][:, c, :], buf[:, l, c, :, idx], reads=[key], writes=["SM3"])
                self.tr(PSB[pi][0:NS, c * 128:(c + 1) * 128], B["SM3"][:, c, :], self.CST[:, 0:128],
                        reads=["SM3", "CST"], writes=[f"ps{pi}"], sig=True)
            rs_ = si % 2
            self.cp("act", YA[0:NS, rs_, 0:256], PSB[pi][0:NS, 0:256], reads=[f"ps{pi}"], writes=[f"YA.{rs_}"])
            self.dma(f"st_row{rs_}", dst[l, :, lastrow, :], YA[0:NS, rs_, 0:256], reads=[f"YA.{rs_}"], is_out=True, eng=AUXQ)

    def ffn_layer(self, l):
        P = self.P
        PSB = self.PSB
        with ExitStack() as es:
            H2 = P.sb(f"H2_{l}", [128, 8, TT], BF16, es)
            ACTG = P.sb(f"ACTG_{l}", [128, 6, TT], BF16, es)
            self.WD = P.sb(f"WD_{l}", [128, 2, 6, D], BF16, es)
            SQ = P.sb(f"SQ2_{l}", [128, 2, 512], BF16, es)
            RS = P.sb(f"RS2_{l}", [128, 512], F32, es)
            SG = P.sb(f"SG_{l}", [128, 2, 512], F32, es)
            tiles = [(512 * t, 512, t) for t in range(4)] + [(T, NS, 4)]
            self.nps_n = 8
            pre_pair = (self.next_piece(), self.next_piece())
            for (c0, n, t) in tiles:
                self.rmsnorm_cols(84, l, (lambda k, c0=c0, n=n: H2[:, k, c0:c0 + n],), f"H2.{t}", c0, n, t, SQ, RS)
            sgi = 0
            for gi, grp in enumerate(FF_GROUPS):
                self.issue_copies(1)
                dk = []
                for li, p in enumerate(grp):
                    if pre_pair is not None:
                        (Wg, gk), (Wu, uk) = pre_pair
                        pre_pair = None
                    else:
                        Wg, gk = self.next_piece()
                        Wu, uk = self.next_piece()
                    for fc in range(2):
                        fi = 2 * li + fc
                        for (c0, n, t) in tiles:
                            hr = [f"H2.{t}.{k}" for k in range(8)]
                            pg = self.nps()
                            for k in range(8):
                                self.mm(PSB[pg][:, 0:n], Wg[:, k, fc * 128:(fc + 1) * 128], H2[:, k, c0:c0 + n],
                                        start=(k == 0), stop=(k == 7), reads=[gk] + hr, writes=[f"ps{pg}"], sig=(k == 7))
                            pu = self.nps()
                            for k in range(8):
                                self.mm(PSB[pu][:, 0:n], Wu[:, k, fc * 128:(fc + 1) * 128], H2[:, k, c0:c0 + n],
                                        start=(k == 0), stop=(k == 7), reads=[uk] + hr, writes=[f"ps{pu}"], sig=(k == 7))
                            s = sgi % 2
                            sgi += 1
                            self.act(SG[:, s, 0:n], PSB[pg][:, 0:n], AF.Silu, reads=[f"ps{pg}"], writes=[f"SG{s}"])
                            self.tt("dve", ACTG[:, fi, c0:c0 + n], PSB[pu][:, 0:n], SG[:, s, 0:n], ALU.mult,
                                    reads=[f"ps{pu}", f"SG{s}"], writes=[f"ACTG.{fi}.{t}"])
                    if li >= 1:
                        _, k_ = self.next_piece()
                        dk.append(k_)
                _, k_ = self.next_piece()
                dk.append(k_)
                nf = 2 * len(grp)
                g2 = gi % 2
                for oc in range(8):
                    for (c0, n, t) in tiles:
                        pi = self.nps()
                        for fi in range(nf):
                            self.mm(PSB[pi][:, 0:n], self.WD[:, g2, fi, oc * 128:(oc + 1) * 128], ACTG[:, fi, c0:c0 + n],
                                    start=(fi == 0), stop=(fi == nf - 1), reads=[dk[fi // 2], f"ACTG.{fi}.{t}"],
                                    writes=[f"ps{pi}"], sig=(fi == nf - 1))
                        self.tt("dve", self.XT[:, oc, c0:c0 + n], PSB[pi][:, 0:n], self.XT[:, oc, c0:c0 + n], ALU.add,
                                reads=[f"ps{pi}", f"XT.{oc}.{t}"], writes=[f"XT.{oc}.{t}"])
            P.drain_dmas()
            P.emit()
            self.WD = None

    def epilogue(self):
        P = self.P
        PSB = self.PSB
        with ExitStack() as es:
            self.nps_n = 8
            self.issue_copies(999)
            FG = P.sb("FG", [128, D], F32, es)
            YO = P.sb("YO", [128, 2, D], F32, es)
            JK = P.sb("JK", [128, 512], F32, es)
            SS = P.sb("SS", [128, 4], F32, es)
            self.dma("ld_fg", FG[:, :], self.fg[:, :], writes=["FG"])
            for a in range(17):
                rows = 128 if a < 16 else NS
                col = 128 * a
                t = a // 4 if a < 16 else 4
                s = a % 2
                pis = []
                for kk in range(2):
                    pi = self.nps()
                    pis.append(pi)
                    for j in range(4):
                        k = 4 * kk + j
                        self.tr(PSB[pi][0:rows, j * 128:(j + 1) * 128], self.XT[:, k, col:col + rows],
                                self.CST[:, 0:128], reads=[f"XT.{k}.{t}", "CST"], writes=[f"ps{pi}"], sig=(j == 3))
                    self.act(JK[0:rows, :], PSB[pi][0:rows, :], AF.Square, reads=[f"ps{pi}"], writes=["JK"],
                             accum_out=SS[0:rows, kk:kk + 1])
                self.tt("dve", SS[0:rows, 2:3], SS[0:rows, 0:1], SS[0:rows, 1:2], ALU.add, reads=["JK"], writes=["SS"])
                self.act(SS[0:rows, 2:3], SS[0:rows, 2:3], AF.Ln, reads=["SS"], writes=["SS"], scale=1.0 / D,
                         bias=self.EPSC[0:rows, 0:1])
                self.act(SS[0:rows, 3:4], SS[0:rows, 2:3], AF.Exp, reads=["SS"], writes=["SS"], scale=-0.5)
                for kk in range(2):
                    self.stt(YO[0:rows, s, 512 * kk:512 * kk + 512], PSB[pis[kk]][0:rows, :], SS[0:rows, 3:4],
                             FG[0:rows, 512 * kk:512 * kk + 512], ALU.mult, ALU.mult,
                             reads=[f"ps{pis[kk]}", "SS", "FG"], writes=[f"YO{s}"])
                dst = self.yp[col:col + 128, :] if a < 16 else self.ys[:, :]
                self.dma(f"st_y{s}", dst, YO[0:rows, s, :], reads=[f"YO{s}"], is_out=True)
            P.wait_all("sp", self.out_toks)
            P.wait_all("pool", self.out_toks)
            P.emit()

    def build(self):
        nc = self.nc
        self.declare()
        with self.es as es:
            P = self.P = Prog(nc, es)
            self.PSB = [P.ps(f"psb{i}", [128, 512]) for i in range(8)]
            self.XT = P.sb("XT", [128, 8, TT], F32)
            self.WSTG = P.sb("WSTG", [128, NSTG, 2048], F32)
            self.WBF = P.sb("WBF", [128, NRING, 2048], BF16)
            self.CST = P.sb("CST", [128, NCST], F32)
            self.PRM = P.sb("PRM", [128, 2 * NPRM], F32)
            self.POOLW = P.sb("POOLW", [128, 512], BF16)
            self.IDB = P.sb("IDB", [128, 128], BF16)
            self.MSK = P.sb("MSK", [128, 384], BF16)
            self.ONESB = P.sb("ONESB", [128, 128], BF16)
            self.EPSC = P.sb("EPSC", [128, 1], F32)
            self.SAT = P.sb("SAT", [128, 2, 2, NS, 31], F32)
            self.SCT = P.sb("SCT", [128, 2, 2, NS, 3], F32)
            self.SPT = P.sb("SPT", [128, 2, 2, NS, 16], F32)
            self.ONESF = self.CST[:, 128:256]
            self.ONESF_T = P.sb("ONESF", [128, 128], F32)
            self.ONESF = self.ONESF_T
            self.build_pieces()
            self.memset("pool", self.ONESF_T[:, :], 1.0, writes=["ONESF"])
            self.prologue()
            st = 0
            for l in range(2):
                for fn in (self.mixer_layer, self.ffn_layer):
                    st += 1
                    if st <= self.stages:
                        fn(l)
            self.epilogue()
        return nc


def _consts():
    c = np.zeros((128, NCST), np.float32)
    p = np.arange(128)
    c[:, 0:128] = np.eye(128, dtype=np.float32)
    c[:, 128:256] = (p[:, None] // 64 == p[None, :] // 64).astype(np.float32)
    c[:, 256:384] = 1.0 / 256.0
    q = np.arange(128)
    c[:, 384:512] = np.where(p[:, None] >= q[None, :], 0.0, NEGM)
    c[:, 512:640] = np.where(p[:, None] <= q[None, :], 0.0, NEGM)
    for b in range(4):
        i = np.arange(32)
        m = np.where((p[:, None] < 32 * b) | ((p[:, None] - 32 * b) <= i[None, :]), 0.0, NEGM)
        c[:, 640 + 32 * b:672 + 32 * b] = m
    t = np.arange(16)
    for ch in range(2):
        w = np.where(p < 64, (2, 8)[ch], (4, 16)[ch]).astype(np.float32)
        c[:, 768 + 16 * ch:784 + 16 * ch] = 1.0 / np.minimum(w[:, None], (t + 1)[None, :].astype(np.float32))
    return c


def _params(inp):
    prm = np.zeros((128, 2 * NPRM), np.float32)
    for l in range(2):
        o = l * NPRM
        for c in range(2):
            sl = slice(c * 128, c * 128 + 128)
            prm[:, o + 31 * c:o + 31 * c + 31] = inp["conv_a_w"][l][:, sl].T
            prm[:, o + 62 + c] = inp["conv_a_b"][l][sl]
            prm[:, o + 64 + c] = inp["ln_a_g"][l][sl]
            prm[:, o + 66 + c] = inp["ln_a_b"][l][sl]
            prm[:, o + 68 + 3 * c:o + 71 + 3 * c] = inp["conv_c_w"][l][:, sl].T
            prm[:, o + 74 + c] = inp["pool_scale"][l][sl]
        prm[:, o + 76:o + 84] = inp["norm1_g"][l].reshape(8, 128).T
        prm[:, o + 84:o + 92] = inp["norm2_g"][l].reshape(8, 128).T
    pw = np.zeros((128, 2, 2, 128), np.float32)
    for l in range(2):
        for c in range(2):
            for hh in range(2):
                g = 2 * c + hh
                pw[64 * hh:64 * hh + 64, l, c, 64 * hh:64 * hh + 64] = inp["pool_w"][l][g]
    return prm, pw.reshape(128, 512)


_NC_CACHE = {}


def _pack_weights(inp):
    wpc = np.empty((92, 128, 2048), np.float32)
    for l in range(2):
        o = 46 * l
        wi = inp["w_in"][l].reshape(8, 128, NPJ)
        wo = inp["w_out"][l].reshape(8, 128, D)
        wg = inp["w_gu"][l].reshape(8, 128, 2 * FF)
        for j in range(9):
            wpc[o + j] = wi[:, :, 256 * j:256 * j + 256].transpose(1, 0, 2).reshape(128, 2048)
        for j in range(4):
            wpc[o + 9 + j] = wo[:, :, 256 * j:256 * j + 256].transpose(1, 0, 2).reshape(128, 2048)
        for p in range(11):
            wpc[o + 13 + 2 * p] = wg[:, :, 256 * p:256 * p + 256].transpose(1, 0, 2).reshape(128, 2048)
            wpc[o + 14 + 2 * p] = wg[:, :, FF + 256 * p:FF + 256 * p + 256].transpose(1, 0, 2).reshape(128, 2048)
            wpc[o + 35 + p] = inp["w_down"][l][256 * p:256 * p + 256, :].reshape(2, 128, D).transpose(1, 0, 2).reshape(128, 2048)
    return wpc


def make_in_maps(inp):
    wpc = _pack_weights(inp)
    prm, pw = _params(inp)
    cst = _consts()
    fg = np.ascontiguousarray(np.broadcast_to(inp["final_g"].astype(np.float32)[None, :], (128, D)))
    f = lambda a: np.ascontiguousarray(a, dtype=np.float32)
    in_maps = []
    for i in range(8):
        sl = slice(NS * i, NS * i + NS)
        in_maps.append({
            "xp": f(inp["x_prompt"][i]),
            "xs": f(inp["x_sample"][sl, 0, :]),
            "ck": f(inp["cache_win_k"][:, sl].reshape(2, NS, 2048, 256)),
            "cv": f(inp["cache_win_v"][:, sl].reshape(2, NS, 2048, 256)),
            "sa": f(inp["state_conv_a"][:, sl]),
            "sc": f(inp["state_conv_c"][:, sl]),
            "spl": f(inp["state_pool"][:, sl]),
            "wpc": wpc,
            "prm": prm, "poolw": pw, "fg": fg, "cst": cst,
        })
    return in_maps


def kernel(**inp):
    inp = {k: np.asarray(v) for k, v in inp.items()}
    if "nc" not in _NC_CACHE:
        _NC_CACHE["nc"] = Kern().build()
    nc = _NC_CACHE["nc"]
    in_maps = make_in_maps(inp)
    res = run_bass_kernel_spmd(nc, in_maps, core_ids=list(range(8)))
    R = res.results
    cat = lambda k, ax: np.concatenate([r[k] for r in R], axis=ax)
    y_p = np.stack([r["yp"] for r in R], 0)
    y_s = cat("ys", 0).reshape(32, 1, D)
    k_p = np.stack([r["kp"] for r in R], 1).reshape(2, 8, T, 4, 64)
    v_p = np.stack([r["vp"] for r in R], 1).reshape(2, 8, T, 4, 64)
    a_p = np.stack([r["apo"] for r in R], 1)
    c_p = np.stack([r["cpo"] for r in R], 1)
    p_p = np.stack([r["ppo"] for r in R], 1)
    k_s = cat("ks", 1).reshape(2, 32, 2048, 4, 64)
    v_s = cat("vs", 1).reshape(2, 32, 2048, 4, 64)
    a_s = cat("aso", 1)
    c_s = cat("cso", 1)
    p_s = cat("pso", 1)
    return (y_p, y_s, k_p, v_p, a_p, c_p, p_p, k_s, v_s, a_s, c_s, p_s)
```
